# Optimizing a Trainium2 kernel written in Bass

```python
import jax, jax.numpy as jnp
from jax import lax
import numpy as np

D_MODEL = 1024
BATCH = 8
SEQ = 2048
DEPTH = 4

GRID_W = 64
CTX_LEN = 256
N_MIXERS = 4
EPS = 1e-6
F32 = jnp.float32

ML_HEADS = 4
ML_DK = 256
ML_DV = 512
ML_QK = ML_HEADS * ML_DK
ML_INNER = ML_HEADS * ML_DV
ML_CHUNK = 64
ML_SPLITS = (ML_QK, 2 * ML_QK, 2 * ML_QK + ML_INNER, 2 * ML_QK + 2 * ML_INNER, 2 * ML_QK + 3 * ML_INNER)
ML_PROJ = 2 * ML_QK + 3 * ML_INNER + 4 * ML_HEADS

AT_HEADS = 16
AT_KV_HEADS = 4
AT_GROUP = AT_HEADS // AT_KV_HEADS
AT_HEAD_DIM = 64
AT_WINDOW = 128
AT_BLOCK = 128
AT_Q = AT_HEADS * AT_HEAD_DIM
AT_KV = AT_KV_HEADS * AT_HEAD_DIM
AT_SPLITS = (AT_Q, AT_Q + AT_KV, AT_Q + 2 * AT_KV)
AT_PROJ = 2 * AT_Q + 2 * AT_KV
ROPE_BASE = 10000.0

SC_WIDTH = D_MODEL
SC_KSIZE = 3
SC_PROJ = 4 * SC_WIDTH

HG_EXPAND = 128
HG_HEADS = D_MODEL // HG_EXPAND
HG_FDIM = HG_HEADS * HG_EXPAND
HG_IDIM = D_MODEL
HG_HEAD_I = HG_IDIM // HG_HEADS
HG_CHUNK = 64
HG_SPLITS = (HG_FDIM, 2 * HG_FDIM, 3 * HG_FDIM, 3 * HG_FDIM + HG_IDIM)
HG_PROJ = 3 * HG_FDIM + 2 * HG_IDIM

kernel_name = 'hybrid_interleaved_mlstm_swa_conv_hgrn2_dit'


def _layers_of(kind):
    return len(range(kind, DEPTH, N_MIXERS))


def rms_norm(a, g):
    af = a.astype(F32)
    af = af * lax.rsqrt(jnp.mean(af * af, axis=-1, keepdims=True) + EPS)
    return (af * g.astype(F32)).astype(a.dtype)


def modulate(a, g, shift, scale):
    return rms_norm(a, g) * (1 + scale) + shift


def _split_heads(a, n_heads):
    b, t, w = a.shape
    return a.reshape(b, t, n_heads, w // n_heads).transpose(0, 2, 1, 3)


def _merge_heads(a):
    b, h, t, d = a.shape
    return a.transpose(0, 2, 1, 3).reshape(b, t, h * d)


def _head_rms(a, g):
    h, d = a.shape[1], a.shape[3]
    a = a * lax.rsqrt(jnp.mean(a * a, axis=-1, keepdims=True) + EPS)
    return a * g.astype(F32).reshape(1, h, 1, d)


def _flip_time(a, direction):
    return a if direction == 0 else jnp.flip(a, axis=2)


def _to_chunks(a, chunk):
    b, h, t = a.shape[:3]
    a = a.reshape(b, h, t // chunk, chunk, *a.shape[3:])
    return jnp.moveaxis(a, 2, 0)


def _from_chunks(a):
    nc, b, h, l = a.shape[:4]
    return jnp.moveaxis(a, 0, 2).reshape(b, h, nc * l, *a.shape[4:])


def axial_rope(n_tokens, head_dim):
    rows = n_tokens // GRID_W
    row = jnp.repeat(jnp.arange(rows), GRID_W).astype(F32)
    col = jnp.tile(jnp.arange(GRID_W), rows).astype(F32)
    n_freq = head_dim // 4
    freqs = jnp.power(ROPE_BASE, -jnp.arange(n_freq, dtype=F32) / n_freq)
    ang = jnp.concatenate([row[:, None] * freqs, col[:, None] * freqs], axis=-1)
    return jnp.cos(ang), jnp.sin(ang)


def apply_rope(a, cos, sin):
    a1, a2 = jnp.split(a, 2, axis=-1)
    cos = cos[None, :, None, :].astype(a.dtype)
    sin = sin[None, :, None, :].astype(a.dtype)
    return jnp.concatenate([a1 * cos - a2 * sin, a1 * sin + a2 * cos], axis=-1)


def sink_softmax(scores, sink):
    m = sink
    for s in scores:
        m = jnp.maximum(m, jnp.max(s, axis=-1))
    exps = [jnp.exp(s - m[..., None]) for s in scores]
    den = jnp.exp(sink - m)
    for e in exps:
        den = den + jnp.sum(e, axis=-1)
    return tuple(e / den[..., None] for e in exps)


def mlstm_scan(q, k, v, log_i, log_f, state):
    causal = jnp.tril(jnp.ones((ML_CHUNK, ML_CHUNK), dtype=bool))

    def step(carry, inp):
        c_mat, n_vec, m_stab = carry
        qc, kc, vc, lic, lfc = inp
        b = jnp.cumsum(lfc, axis=-1)
        log_d = jnp.where(causal, b[..., :, None] - b[..., None, :] + lic[..., None, :], -jnp.inf)
        log_inter = b + m_stab[..., None]
        m_row = jnp.maximum(log_inter, jnp.max(log_d, axis=-1))
        inter = jnp.exp(log_inter - m_row)
        s = jnp.einsum('bhtd,bhsd->bhts', qc, kc) * jnp.exp(log_d - m_row[..., None])
        num = inter[..., None] * jnp.einsum('bhtd,bhde->bhte', qc, c_mat) + jnp.einsum('bhts,bhse->bhte', s, vc)
        den = inter * jnp.einsum('bhtd,bhd->bht', qc, n_vec) + jnp.sum(s, axis=-1)
        h = num / jnp.maximum(jnp.abs(den), jnp.exp(-m_row))[..., None]
        b_last = b[..., -1]
        log_w = b_last[..., None] - b + lic
        m_new = jnp.maximum(b_last + m_stab, jnp.max(log_w, axis=-1))
        decay = jnp.exp(b_last + m_stab - m_new)
        kw = kc * jnp.exp(log_w - m_new[..., None])[..., None]
        c_new = decay[..., None, None] * c_mat + jnp.einsum('bhsd,bhse->bhde', kw, vc)
        n_new = decay[..., None] * n_vec + jnp.sum(kw, axis=2)
        return (c_new, n_new, m_new), h

    xs = tuple(_to_chunks(a.astype(F32), ML_CHUNK) for a in (q, k, v, log_i, log_f))
    state, h = lax.scan(step, state, xs)
    return _from_chunks(h), state


def mlstm_mixer(h_lat, h_ctx, w_in, gate_b, head_g, w_out, need_ctx):
    def project(h):
        b, t, _ = h.shape
        q, k, v, o, z, g = jnp.split(h @ w_in, ML_SPLITS, axis=-1)
        q = _split_heads(q, ML_HEADS).astype(F32) * (ML_DK ** -0.5)
        k = _split_heads(k, ML_HEADS).astype(F32)
        v = _split_heads(v, ML_HEADS).astype(F32)
        g = (g.reshape(b, t, 4, ML_HEADS) + gate_b).astype(F32).transpose(2, 0, 3, 1)
        return q, k, v, g, o, z

    ql, kl, vl, gl, ol, zl = project(h_lat)
    qc, kc, vc, gc, oc, zc = project(h_ctx)
    bsz = h_lat.shape[0]
    zero = (jnp.zeros((bsz, ML_HEADS, ML_DK, ML_DV), F32),
            jnp.zeros((bsz, ML_HEADS, ML_DK), F32),
            jnp.zeros((bsz, ML_HEADS), F32))
    h_lat_dirs, h_ctx_dirs = [], []
    for d in range(2):
        li_c, lf_c = gc[2 * d], jax.nn.log_sigmoid(gc[2 * d + 1])
        li_l, lf_l = gl[2 * d], jax.nn.log_sigmoid(gl[2 * d + 1])
        hc, st = mlstm_scan(_flip_time(qc, d), _flip_time(kc, d), _flip_time(vc, d),
                            _flip_time(li_c, d), _flip_time(lf_c, d), zero)
        hl, _ = mlstm_scan(_flip_time(ql, d), _flip_time(kl, d), _flip_time(vl, d),
                           _flip_time(li_l, d), _flip_time(lf_l, d), st)
        h_lat_dirs.append(_flip_time(hl, d))
        h_ctx_dirs.append(_flip_time(hc, d))

    def finish(h_sum, o, z):
        hn = _merge_heads(_head_rms(h_sum, head_g)).astype(o.dtype)
        return (jax.nn.sigmoid(o) * hn * jax.nn.silu(z)) @ w_out

    y_lat = finish(h_lat_dirs[0] + h_lat_dirs[1], ol, zl)
    y_ctx = finish(h_ctx_dirs[0] + h_ctx_dirs[1], oc, zc) if need_ctx else None
    return y_lat, y_ctx


def attn_mixer(h_lat, h_ctx, w_in, q_g, k_g, sink, w_out, need_ctx):
    bsz, t_lat, _ = h_lat.shape
    scale = AT_HEAD_DIM ** -0.5
    sink_f = sink.astype(F32).reshape(AT_KV_HEADS, AT_GROUP)

    def project(h):
        b, t, _ = h.shape
        q, k, v, z = jnp.split(h @ w_in, AT_SPLITS, axis=-1)
        q = rms_norm(q.reshape(b, t, AT_HEADS, AT_HEAD_DIM), q_g)
        k = rms_norm(k.reshape(b, t, AT_KV_HEADS, AT_HEAD_DIM), k_g)
        return q, k, v.reshape(b, t, AT_KV_HEADS, AT_HEAD_DIM), z

    ql, kl, vl, zl = project(h_lat)
    qc, kc, vc, zc = project(h_ctx)
    cos, sin = axial_rope(t_lat, AT_HEAD_DIM)
    ql = apply_rope(ql, cos, sin) * scale
    kl = apply_rope(kl, cos, sin)

    nb = t_lat // AT_BLOCK
    qb = ql.reshape(bsz, nb, AT_BLOCK, AT_KV_HEADS, AT_GROUP, AT_HEAD_DIM)
    pad = ((0, 0), (AT_BLOCK, AT_BLOCK), (0, 0), (0, 0))
    kp = jnp.pad(kl, pad).reshape(bsz, nb + 2, AT_BLOCK, AT_KV_HEADS, AT_HEAD_DIM)
    vp = jnp.pad(vl, pad).reshape(bsz, nb + 2, AT_BLOCK, AT_KV_HEADS, AT_HEAD_DIM)
    kb = jnp.concatenate([kp[:, :-2], kp[:, 1:-1], kp[:, 2:]], axis=2)
    vb = jnp.concatenate([vp[:, :-2], vp[:, 1:-1], vp[:, 2:]], axis=2)
    q_pos = jnp.arange(t_lat).reshape(nb, AT_BLOCK)
    k_pos = (jnp.arange(nb)[:, None] - 1) * AT_BLOCK + jnp.arange(3 * AT_BLOCK)[None, :]
    band = ((jnp.abs(q_pos[:, :, None] - k_pos[:, None, :]) <= AT_WINDOW)
            & (k_pos[:, None, :] >= 0) & (k_pos[:, None, :] < t_lat))
    s_loc = jnp.einsum('bnqkgd,bnskd->bnkgqs', qb, kb).astype(F32)
    s_loc = jnp.where(band[None, :, None, None], s_loc, -jnp.inf)
    s_ctx = jnp.einsum('bnqkgd,bckd->bnkgqc', qb, kc).astype(F32)
    p_loc, p_ctx = sink_softmax((s_loc, s_ctx), sink_f[None, None, :, :, None])
    o = (jnp.einsum('bnkgqs,bnskd->bnqkgd', p_loc, vb.astype(F32))
         + jnp.einsum('bnkgqc,bckd->bnqkgd', p_ctx, vc.astype(F32)))
    o = o.reshape(bsz, t_lat, AT_Q).astype(h_lat.dtype)
    y_lat = (o * jax.nn.silu(zl)) @ w_out

    y_ctx = None
    if need_ctx:
        qcs = (qc * scale).reshape(bsz, -1, AT_KV_HEADS, AT_GROUP, AT_HEAD_DIM)
        s_cc = jnp.einsum('bqkgd,bskd->bkgqs', qcs, kc).astype(F32)
        (p_cc,) = sink_softmax((s_cc,), sink_f[None, :, :, None])
        oc = jnp.einsum('bkgqs,bskd->bqkgd', p_cc, vc.astype(F32)).reshape(bsz, -1, AT_Q).astype(h_ctx.dtype)
        y_ctx = (oc * jax.nn.silu(zc)) @ w_out
    return y_lat, y_ctx


def _dwconv(a, w, b):
    y = lax.conv_general_dilated(a, w[:, None, :].astype(a.dtype), window_strides=(1,),
                                 padding=((SC_KSIZE // 2, SC_KSIZE // 2),),
                                 dimension_numbers=('NWC', 'WIO', 'NWC'),
                                 feature_group_count=a.shape[-1])
    return y + b.astype(a.dtype)


def conv_mixer(h_lat, h_ctx, w_in, conv_w, conv_b, w_out, need_ctx):
    def run(h):
        xin, b_gate, c_gate, z = jnp.split(h @ w_in, 4, axis=-1)
        y = _dwconv(c_gate * xin, conv_w, conv_b)
        return (b_gate * y * jax.nn.silu(z)) @ w_out
    return run(h_lat), (run(h_ctx) if need_ctx else None)


def hgrn_scan(q, k, i, log_f, state):
    causal = jnp.tril(jnp.ones((HG_CHUNK, HG_CHUNK), dtype=bool))[:, :, None]

    def step(s_mat, inp):
        qc, kc, ic, lfc = inp
        a = jnp.cumsum(lfc, axis=2)
        rel = jnp.where(causal, a[:, :, :, None, :] - a[:, :, None, :, :], -jnp.inf)
        attn = jnp.einsum('bhtf,bhsf,bhtsf->bhts', qc, kc, jnp.exp(rel))
        o = jnp.einsum('bhts,bhsi->bhti', attn, ic) + jnp.einsum('bhtf,bhfi->bhti', qc * jnp.exp(a), s_mat)
        a_last = a[:, :, -1]
        kd = kc * jnp.exp(a_last[:, :, None, :] - a)
        s_new = jnp.exp(a_last)[..., None] * s_mat + jnp.einsum('bhsf,bhsi->bhfi', kd, ic)
        return s_new, o

    xs = tuple(_to_chunks(t.astype(F32), HG_CHUNK) for t in (q, k, i, log_f))
    state, o = lax.scan(step, state, xs)
    return _from_chunks(o), state


def hgrn_mixer(h_lat, h_ctx, w_in, f_b, lb_param, head_g, w_out, layer, need_ctx):
    p = jax.nn.softmax(lb_param.astype(F32), axis=1)
    lb = (jnp.cumsum(p, axis=1) - p[:, :1])[:, layer]

    def project(h):
        q, f_fw, f_bw, i, z = jnp.split(h @ w_in, HG_SPLITS, axis=-1)
        q = _split_heads(jax.nn.silu(q).astype(F32), HG_HEADS)
        i = _split_heads(i.astype(F32), HG_HEADS)
        dirs = []
        for d, fp in enumerate((f_fw, f_bw)):
            f = lb[d] + (1.0 - lb[d]) * jax.nn.sigmoid(fp.astype(F32) + f_b[d].astype(F32))
            dirs.append((_split_heads(1.0 - f, HG_HEADS), _split_heads(jnp.log(f), HG_HEADS)))
        return q, i, dirs, z

    ql, il, dl, zl = project(h_lat)
    qc, ic, dc, zc = project(h_ctx)
    s0 = jnp.zeros((h_lat.shape[0], HG_HEADS, HG_EXPAND, HG_HEAD_I), F32)
    o_lat, o_ctx = [], []
    for d in range(2):
        oc_d, s_ctx = hgrn_scan(_flip_time(qc, d), _flip_time(dc[d][0], d), _flip_time(ic, d),
                                _flip_time(dc[d][1], d), s0)
        ol_d, _ = hgrn_scan(_flip_time(ql, d), _flip_time(dl[d][0], d), _flip_time(il, d),
                            _flip_time(dl[d][1], d), s_ctx)
        o_lat.append(_flip_time(ol_d, d))
        o_ctx.append(_flip_time(oc_d, d))

    def finish(o, z):
        hn = _merge_heads(_head_rms(o, head_g)).astype(z.dtype)
        return (hn * jax.nn.silu(z)) @ w_out

    y_lat = finish(o_lat[0] + o_lat[1], zl)
    y_ctx = finish(o_ctx[0] + o_ctx[1], zc) if need_ctx else None
    return y_lat, y_ctx


def setup_inputs(seed: int = 0) -> dict:
    key = jax.random.key(seed)
    ks = iter(jax.random.split(key, 40))

    def nrm(shape, scale):
        return scale * jax.random.normal(next(ks), shape, F32)

    n_a, n_b, n_c, n_d = (_layers_of(m) for m in range(N_MIXERS))
    d = D_MODEL
    f_bias = jnp.linspace(3.0, 6.0, ML_HEADS, dtype=F32)
    ml_gate_b = jnp.stack([nrm((n_a, ML_HEADS), 0.1), f_bias + nrm((n_a, ML_HEADS), 0.1),
                           nrm((n_a, ML_HEADS), 0.1), f_bias + nrm((n_a, ML_HEADS), 0.1)], axis=1)
    return {
        'x': nrm((BATCH, SEQ, d), 1.0),
        'c': nrm((BATCH, d), 1.0),
        'ctx': nrm((BATCH, CTX_LEN, d), 1.0),
        'c_ctx': nrm((d,), 1.0),
        'ada_w': nrm((DEPTH, d, 3 * d), 0.5 * d ** -0.5),
        'ada_b': nrm((DEPTH, 3 * d), 0.02),
        'norm_g': 1.0 + nrm((DEPTH, d), 0.02),
        'ml_w_in': nrm((n_a, d, ML_PROJ), d ** -0.5),
        'ml_gate_b': ml_gate_b,
        'ml_head_g': 1.0 + nrm((n_a, ML_INNER), 0.02),
        'ml_w_out': nrm((n_a, ML_INNER, d), ML_INNER ** -0.5),
        'at_w_in': nrm((n_b, d, AT_PROJ), d ** -0.5),
        'at_q_g': 1.0 + nrm((n_b, AT_HEAD_DIM), 0.02),
        'at_k_g': 1.0 + nrm((n_b, AT_HEAD_DIM), 0.02),
        'at_sink': nrm((n_b, AT_HEADS), 1.0),
        'at_w_out': nrm((n_b, AT_Q, d), AT_Q ** -0.5),
        'sc_w_in': nrm((n_c, d, SC_PROJ), d ** -0.5),
        'sc_conv_w': nrm((n_c, SC_KSIZE, SC_WIDTH), SC_KSIZE ** -0.5),
        'sc_conv_b': nrm((n_c, SC_WIDTH), 0.02),
        'sc_w_out': nrm((n_c, SC_WIDTH, d), SC_WIDTH ** -0.5),
        'hg_w_in': nrm((n_d, d, HG_PROJ), d ** -0.5),
        'hg_f_b': nrm((n_d, 2, HG_FDIM), 0.1),
        'hg_lb': nrm((n_d, 2, DEPTH, HG_FDIM), 0.1),
        'hg_head_g': 1.0 + nrm((n_d, HG_IDIM), 0.02),
        'hg_w_out': nrm((n_d, HG_IDIM, d), HG_IDIM ** -0.5),
    }


def reference(x, c, ctx, c_ctx, ada_w, ada_b, norm_g,
              ml_w_in, ml_gate_b, ml_head_g, ml_w_out,
              at_w_in, at_q_g, at_k_g, at_sink, at_w_out,
              sc_w_in, sc_conv_w, sc_conv_b, sc_w_out,
              hg_w_in, hg_f_b, hg_lb, hg_head_g, hg_w_out):
    for layer in range(DEPTH):
        kind, j = layer % N_MIXERS, layer // N_MIXERS
        need_ctx = layer < DEPTH - 1
        mod_lat = jax.nn.silu(c) @ ada_w[layer] + ada_b[layer]
        mod_ctx = jax.nn.silu(c_ctx) @ ada_w[layer] + ada_b[layer]
        sh_l, sc_l, g_l = jnp.split(mod_lat[:, None, :], 3, axis=-1)
        sh_c, sc_c, g_c = jnp.split(mod_ctx[None, None, :], 3, axis=-1)
        h_lat = modulate(x, norm_g[layer], sh_l, sc_l)
        h_ctx = modulate(ctx, norm_g[layer], sh_c, sc_c)
        if kind == 0:
            y_lat, y_ctx = mlstm_mixer(h_lat, h_ctx, ml_w_in[j], ml_gate_b[j], ml_head_g[j], ml_w_out[j], need_ctx)
        elif kind == 1:
            y_lat, y_ctx = attn_mixer(h_lat, h_ctx, at_w_in[j], at_q_g[j], at_k_g[j], at_sink[j], at_w_out[j], need_ctx)
        elif kind == 2:
            y_lat, y_ctx = conv_mixer(h_lat, h_ctx, sc_w_in[j], sc_conv_w[j], sc_conv_b[j], sc_w_out[j], need_ctx)
        else:
            y_lat, y_ctx = hgrn_mixer(h_lat, h_ctx, hg_w_in[j], hg_f_b[j], hg_lb[j], hg_head_g[j], hg_w_out[j],
                                      layer, need_ctx)
        x = x + g_l * y_lat
        if need_ctx:
            ctx = ctx + g_c * y_ctx
    return x
```

```python
import numpy as np
from contextlib import ExitStack
import concourse.bass as bass
import concourse.mybir as mybir
from concourse.bass_utils import run_bass_kernel_spmd

F32 = mybir.dt.float32
BF16 = mybir.dt.bfloat16
AF = mybir.ActivationFunctionType
ALU = mybir.AluOpType
AX = mybir.AxisListType

D = 1024
T_LAT = 2048
T_CTX = 256
T_ALL = T_LAT + T_CTX
NT = T_ALL // 128
NT_CTX = T_CTX // 128
EPS = 1e-6
KC = D // 128

COMPUTE = ('pe', 'act', 'dve', 'pool')
N_DMA_SEMS = 24


class _Op:
    __slots__ = ('eng', 'fn', 'deps', 'idx', 'is_dma', 'sig', 'prewait', 'need')


class Sched:
    def __init__(self, nc, es):
        self.nc = nc
        self.ops = []
        self.last_w = {}
        self.readers = {}
        self.last_on_eng = {}
        self.dma_since_bar = []
        self.engs = {'pe': nc.tensor, 'act': nc.scalar, 'dve': nc.vector,
                     'pool': nc.gpsimd, 'sp': nc.sync}
        self.esem = {e: es.enter_context(nc.semaphore('sem_' + e)) for e in COMPUTE}
        self.dsem = [es.enter_context(nc.semaphore('dsem%d' % i)) for i in range(N_DMA_SEMS)]

    def add(self, eng, fn, r=(), w=(), is_dma=False, extra=()):
        op = _Op()
        op.eng = eng
        op.fn = fn
        op.idx = len(self.ops)
        op.is_dma = is_dma
        op.sig = None
        op.prewait = None
        op.need = False
        deps = set(extra)
        for k in r:
            lw = self.last_w.get(k)
            if lw is not None:
                deps.add(lw)
        for k in w:
            lw = self.last_w.get(k)
            if lw is not None:
                deps.add(lw)
            rd = self.readers.get(k)
            if rd:
                for v in rd.values():
                    if isinstance(v, list):
                        deps.update(v)
                    else:
                        deps.add(v)
        for k in w:
            self.last_w[k] = op.idx
            self.readers[k] = {}
        for k in r:
            if k in w:
                continue
            rd = self.readers.setdefault(k, {})
            if is_dma:
                rd.setdefault('dma', []).append(op.idx)
            else:
                rd[eng] = op.idx
        deps.discard(op.idx)
        op.deps = deps
        self.ops.append(op)
        if is_dma:
            self.dma_since_bar.append(op.idx)
        else:
            self.last_on_eng[eng] = op.idx
        return op

    def pe(self, fn, r=(), w=()):
        return self.add('pe', fn, r, w)

    def act(self, fn, r=(), w=()):
        return self.add('act', fn, r, w)

    def dve(self, fn, r=(), w=()):
        return self.add('dve', fn, r, w)

    def pool(self, fn, r=(), w=()):
        return self.add('pool', fn, r, w)

    def dma(self, out, in_, r=(), w=(), **kw):
        return self.add('sp', lambda e: e.dma_start(out=out, in_=in_, **kw), r, w, is_dma=True)

    def barrier(self):
        ex = set(self.last_on_eng.values()) | set(self.dma_since_bar)
        for e in COMPUTE + ('sp',):
            self.add(e, None, extra=ex)
        self.dma_since_bar = []
        self.last_w = {}
        self.readers = {}

    def emit(self):
        ops = self.ops
        for op in ops:
            for d in op.deps:
                if op.eng == 'pe' and ops[d].eng == 'pe':
                    continue
                ops[d].need = True
        cnt = {e: 0 for e in COMPUTE}
        dval = [0] * N_DMA_SEMS
        ndma = 0
        for op in ops:
            if op.is_dma:
                j = ndma % N_DMA_SEMS
                ndma += 1
                if dval[j] > 0:
                    op.prewait = (('d', j), self.dsem[j], dval[j])
                dval[j] += 16
                op.sig = (('d', j), self.dsem[j], dval[j])
            elif op.need and op.fn is not None:
                cnt[op.eng] += 1
                op.sig = (op.eng, self.esem[op.eng], cnt[op.eng])
        know = {e: {} for e in self.engs}
        clocks = [None] * len(ops)
        nwait = 0
        for op in ops:
            e = self.engs[op.eng]
            kn = know[op.eng]
            waits = []
            for d in sorted(op.deps, reverse=True):
                dop = ops[d]
                if dop.sig is None:
                    continue
                key, sem, val = dop.sig
                if op.eng == 'pe' and dop.eng == 'pe' and not dop.is_dma:
                    continue
                if kn.get(key, 0) >= val:
                    continue
                waits.append((key, sem, val))
                ck = clocks[d]
                if ck:
                    for k2, v2 in ck.items():
                        if kn.get(k2, 0) < v2:
                            kn[k2] = v2
                kn[key] = max(kn.get(key, 0), val)
            if op.prewait is not None:
                key, sem, val = op.prewait
                if kn.get(key, 0) < val:
                    waits.append((key, sem, val))
                    kn[key] = val
            best = {}
            for key, sem, val in waits:
                if key not in best or best[key][1] < val:
                    best[key] = (sem, val)
            for key, (sem, val) in best.items():
                e.wait_ge(sem, val)
                nwait += 1
            if op.fn is not None:
                ins = op.fn(e)
                if op.sig is not None:
                    key, sem, val = op.sig
                    ins.then_inc(sem, 16 if op.is_dma else 1)
                    ck = dict(kn)
                    ck[key] = val
                    clocks[op.idx] = ck
                    if not op.is_dma:
                        pass
        sp = self.engs['sp']
        for j in range(N_DMA_SEMS):
            if dval[j] > 0 and know['sp'].get(('d', j), 0) < dval[j]:
                sp.wait_ge(self.dsem[j], dval[j])
        return len(ops), nwait, cnt, ndma


W_SPECS = [
    ('ada_w', [4, D, 3 * D]), ('ada_b', [4, 3 * D]), ('norm_g', [4, D]),
    ('ml_w_in', [D, 8208]), ('ml_gate_b', [1, 16]), ('ml_head_g', [1, 2048]), ('ml_w_out', [2048, D]),
    ('at_w_in', [D, 2560]), ('at_q_g', [1, 64]), ('at_k_g', [1, 64]), ('at_sink', [1, 16]), ('at_w_out', [D, D]),
    ('sc_w_in', [D, 4096]), ('sc_conv_w', [3, D]), ('sc_conv_b', [1, D]), ('sc_w_out', [D, D]),
    ('hg_w_in', [D, 5120]), ('hg_f_b', [2, D]), ('hg_lb', [2, 4, D]), ('hg_head_g', [1, D]), ('hg_w_out', [D, D]),
]


class Prog:
    def __init__(self, layers):
        self.layers = layers
        self.nc = nc = bass.Bass("TRN2", target_bir_lowering=False)
        self.es = ExitStack()
        self.dr = {}
        dr = self.dr
        dr['x'] = nc.dram_tensor("x", [T_LAT, D], F32, kind="ExternalInput").ap()
        dr['ctx'] = nc.dram_tensor("ctx", [T_CTX, D], F32, kind="ExternalInput").ap()
        dr['c2'] = nc.dram_tensor("c2", [2, D], F32, kind="ExternalInput").ap()
        for name, shape in W_SPECS:
            dr[name] = nc.dram_tensor(name, shape, F32, kind="ExternalInput").ap()
        dr['ident'] = nc.dram_tensor("ident", [128, 128], F32, kind="ExternalInput").ap()
        dr['cf'] = nc.dram_tensor("cf", [128, 256], F32, kind="ExternalInput").ap()
        dr['cb'] = nc.dram_tensor("cb", [128, 256], F32, kind="ExternalInput").ap()
        dr['osc0'] = nc.dram_tensor("osc0", [T_ALL, D], F32, kind="Internal").ap()
        dr['osc1'] = nc.dram_tensor("osc1", [T_ALL, D], F32, kind="Internal").ap()
        dr['sel'] = nc.dram_tensor("sel", [2, 256], F32, kind="ExternalInput").ap()
        dr['rope'] = nc.dram_tensor("rope", [T_LAT, 64], F32, kind="ExternalInput").ap()
        dr['out'] = nc.dram_tensor("out", [T_LAT, D], F32, kind="ExternalOutput").ap()
        dr['ctxs'] = nc.dram_tensor("ctxs", [T_CTX, D], F32, kind="ExternalOutput").ap()
        dr['ut'] = nc.dram_tensor("ut_scr", [2048, T_ALL], BF16, kind="Internal").ap()

    def sb(self, es, name, shape, dt):
        self._uid = getattr(self, '_uid', 0) + 1
        return es.enter_context(self.nc.sbuf_tensor("sb%d_%s" % (self._uid, name), shape, dt))

    def build(self):
        nc = self.nc
        es = self.es
        with es:
            self.S = S = Sched(nc, es)
            self.PS = [es.enter_context(nc.psum_tensor("ps%d" % i, [128, 512], F32)) for i in range(8)]
            self._consts(es)
            for li, l in enumerate(self.layers):
                with ExitStack() as les:
                    self._layer(les, l, first=(li == 0))
                    S.barrier()
                    stats = None
            stats = S.emit()
        self.stats = stats
        return nc

    def _consts(self, es):
        S = self.S
        dr = self.dr
        self.ident_f = self.sb(es, "ident_f", [128, 128], F32)
        self.ident_b = self.sb(es, "ident_b", [128, 128], BF16)
        self.cfb = [self.sb(es, "cf", [128, 256], F32), self.sb(es, "cb", [128, 256], F32)]
        self.ones_f = self.sb(es, "ones_f", [128, 128], F32)
        S.pool(lambda e: e.memset(self.ones_f[:], 1.0), w=['ones_f'])
        self.sel = self.sb(es, "sel", [2, 256], F32)
        self.scT = self.sb(es, "scT", [128, KC, 2], F32)
        S.dma(self.ident_f[:], dr['ident'], w=['ident_f'])
        S.dma(self.cfb[0][:], dr['cf'], w=['cfb'])
        S.dma(self.cfb[1][:], dr['cb'], w=['cfb'])
        S.dma(self.sel[:], dr['sel'], w=['sel'])
        S.dve(lambda e: e.tensor_copy(out=self.ident_b[:], in_=self.ident_f[:]), r=['ident_f'], w=['ident_b'])
        cT = self.sb(es, "cT", [128, KC, 2], F32)
        sg = self.sb(es, "c_sg", [128, KC, 2], F32)
        for r_ in range(2):
            S.dma(cT[:, :, r_], dr['c2'][r_, :].rearrange("(c p) -> p c", p=128), w=['cT'],
                  allow_slow_non_contiguous=True)
        S.act(lambda e: e.activation(out=sg[:], in_=cT[:], func=AF.Sigmoid), r=['cT'], w=['c_sg'])
        S.dve(lambda e: e.tensor_tensor(out=self.scT[:], in0=cT[:], in1=sg[:], op=ALU.mult), r=['cT', 'c_sg'], w=['scT'])

    def res_src(self, t, first):
        dr = self.dr
        if t < NT_CTX:
            base = dr['ctx'] if first else dr['ctxs']
            return base[t * 128:(t + 1) * 128, :]
        base = dr['x'] if first else dr['out']
        tt = t - NT_CTX
        return base[tt * 128:(tt + 1) * 128, :]

    def res_dst(self, t):
        dr = self.dr
        if t < NT_CTX:
            return dr['ctxs'][t * 128:(t + 1) * 128, :]
        tt = t - NT_CTX
        return dr['out'][tt * 128:(tt + 1) * 128, :]

    def load_w(self, dst, dst_key, wdram, c0, n, kc=KC, step=256):
        S = self.S
        for k0 in range(0, kc, 8):
            for j0 in range(0, n, step):
                m = min(step, n - j0)
                i = self._wst_i % len(self.wst)
                self._wst_i += 1
                st = self.wst[i]
                key = 'wst%d' % i
                S.dma(st[:, :, 0:m],
                      wdram[k0 * 128:(k0 + 8) * 128, c0 + j0:c0 + j0 + m].rearrange("(c p) n -> p c n", p=128),
                      w=[key])
                S.pool(lambda e, st=st, m=m, j0=j0, k0=k0: e.tensor_copy(out=dst[:, k0:k0 + 8, j0:j0 + m],
                                                                       in_=st[:, :, 0:m]),
                       r=[key], w=[dst_key])

    def _phase_a(self, les, l, first):
        S = self.S
        dr = self.dr
        PS = self.PS
        nc = self.nc
        self.wst = [self.sb(les, "wst%d" % i, [128, 8, 256], F32) for i in range(2)]
        self._wst_i = 0
        self.hT = hT_ = self.sb(les, "hT", [128, KC, T_ALL], BF16)
        self.gate = gate_ = [self.sb(les, "gate%d" % r, [128, D], F32) for r in range(2)]
        with ExitStack() as tes:
            mod = self.sb(tes, "mod", [2, 3 * D], F32)
            adab = self.sb(tes, "adab", [2, 3 * D], F32)
            self.modb = modb_ = [self.sb(tes, "modb%d" % r, [128, 3 * D], F32) for r in range(2)]
            ng = self.sb(tes, "ng", [128, D], F32)
            S.dma(adab[:], dr['ada_b'][l:l + 1, :].partition_broadcast(2), w=['adab'])
            S.dma(ng[:], dr['norm_g'][l:l + 1, :].partition_broadcast(128), w=['ng'])
            adw = [self.sb(tes, "adw%d" % i, [128, KC, 512], F32) for i in range(2)]
            for n in range(6):
                a = adw[n % 2]
                ak = 'adw%d' % (n % 2)
                S.dma(a[:], dr['ada_w'][l, :, n * 512:(n + 1) * 512].rearrange("(c p) n -> p c n", p=128), w=[ak])
                pk = 'ps0'
                for k in range(KC):
                    S.pe(lambda e, a=a, k=k: e.matmul(PS[0][0:2, :], lhsT=self.scT[:, k, :], rhs=a[:, k, :],
                                                      start=(k == 0), stop=(k == KC - 1)),
                         r=[ak, 'scT'], w=[pk])
                S.dve(lambda e, n=n: e.tensor_tensor(out=mod[:, n * 512:(n + 1) * 512], in0=PS[0][0:2, :],
                                                     in1=adab[:, n * 512:(n + 1) * 512], op=ALU.add),
                      r=[pk, 'adab'], w=['mod'])
            S.dve(lambda e: e.tensor_scalar_add(out=mod[:, D:2 * D], in0=mod[:, D:2 * D], scalar1=1.0),
                  r=['mod'], w=['mod'])
            for r_ in range(2):
                for n in range(6):
                    pk = 'ps%d' % (1 + n % 2)
                    P = PS[1 + n % 2]
                    S.pe(lambda e, P=P, r_=r_, n=n: e.matmul(P[:, :], lhsT=self.sel[:, r_ * 128:(r_ + 1) * 128],
                                                             rhs=mod[:, n * 512:(n + 1) * 512], start=True, stop=True),
                         r=['sel', 'mod'], w=[pk])
                    S.act(lambda e, P=P, r_=r_, n=n: e.copy(out=modb_[r_][:, n * 512:(n + 1) * 512], in_=P[:, :]),
                          r=[pk], w=['modb%d' % r_])
                S.dve(lambda e, r_=r_: e.tensor_tensor(out=modb_[r_][:, D:2 * D], in0=modb_[r_][:, D:2 * D],
                                                       in1=ng[:], op=ALU.mult),
                      r=['modb%d' % r_, 'ng'], w=['modb%d' % r_])
                S.pool(lambda e, r_=r_: e.tensor_copy(out=gate_[r_][:], in_=modb_[r_][:, 2 * D:3 * D]),
                       r=['modb%d' % r_], w=['gate%d' % r_])
            xts = [self.sb(tes, "xt%d" % i, [128, D], F32) for i in range(2)]
            junk = self.sb(tes, "junk", [128, D], BF16)
            h1 = [self.sb(tes, "h1_%d" % i, [128, D], F32) for i in range(2)]
            hb = [self.sb(tes, "hb_%d" % i, [128, D], BF16) for i in range(2)]
            ss = self.sb(tes, "ss", [128, 2 * NT], F32)
            for t in range(NT):
                i = t % 2
                xt = xts[i]
                xk = 'xt%d' % i
                mb = modb_[1 if t < NT_CTX else 0]
                mk = 'modb%d' % (1 if t < NT_CTX else 0)
                S.dma(xt[:], self.res_src(t, first), r=[('res', t)], w=[xk])
                S.act(lambda e, xt=xt, t=t: e.activation(out=junk[:], in_=xt[:], func=AF.Square,
                                                         accum_out=ss[:, 2 * t:2 * t + 1]),
                      r=[xk], w=['junk', ('ss', t)])
                S.act(lambda e, t=t: e.activation(out=ss[:, 2 * t + 1:2 * t + 2], in_=ss[:, 2 * t:2 * t + 1],
                                                  func=AF.Sqrt, scale=1.0 / D, bias=self.eps_col[:, 0:1]),
                      r=[('ss', t), 'eps'], w=[('ss1', t)])
                S.dve(lambda e, t=t: e.reciprocal(out=ss[:, 2 * t:2 * t + 1], in_=ss[:, 2 * t + 1:2 * t + 2]),
                      r=[('ss1', t)], w=[('ss', t)])
                S.dve(lambda e, xt=xt, t=t, i=i, mb=mb: e.scalar_tensor_tensor(
                    out=h1[i][:], in0=xt[:], scalar=ss[:, 2 * t:2 * t + 1], in1=mb[:, D:2 * D],
                    op0=ALU.mult, op1=ALU.mult), r=[xk, ('ss', t), mk], w=['h1_%d' % i])
                S.pool(lambda e, i=i, mb=mb: e.tensor_tensor(out=hb[i][:], in0=h1[i][:], in1=mb[:, 0:D], op=ALU.add),
                       r=['h1_%d' % i, mk], w=['hb_%d' % i])
                pj = 2 + (t % 2)
                Pb = PS[pj][:].bitcast(BF16)
                for k in range(KC):
                    S.pe(lambda e, Pb=Pb, i=i, k=k: e.transpose(out=Pb[:, k * 128:(k + 1) * 128],
                                                                in_=hb[i][:, k * 128:(k + 1) * 128],
                                                                identity=self.ident_b[:]),
                         r=['hb_%d' % i, 'ident_b'], w=['ps%d' % pj])
                S.act(lambda e, Pb=Pb, t=t: e.copy(out=hT_[:, :, t * 128:(t + 1) * 128],
                                                   in_=Pb.rearrange("p (c n) -> p c n", c=KC)),
                      r=['ps%d' % pj], w=[('hT', t)])
            S.barrier()

    def _z_begin(self, tes):
        self._zx = [self.sb(tes, "zx%d" % i, [128, D], F32) for i in range(2)]
        self._ztmp = [self.sb(tes, "ztmp%d" % i, [128, D], F32) for i in range(2)]
        self._zn = 0

    def _z_tile(self, t, uT, ukeys, wout_key, n_kc):
        S = self.S
        PS = self.PS
        xr, tmp = self._zx, self._ztmp
        i = self._zn % 2
        self._zn += 1
        gt = self.gate[1 if t < NT_CTX else 0]
        wout = self.wout
        mk = 'gate%d' % (1 if t < NT_CTX else 0)
        S.dma(xr[i][:], self.res_cur(t), r=[('res', t)], w=['zx%d' % i])
        for half in range(2):
            pj = 4 + 2 * i + half
            P = PS[pj]
            for k in range(n_kc):
                S.pe(lambda e, P=P, k=k, half=half: e.matmul(
                    P[:, :], lhsT=uT[:, k, :], rhs=wout[:, k, half * 512:(half + 1) * 512],
                    start=(k == 0), stop=(k == n_kc - 1)), r=list(ukeys) + [wout_key], w=['ps%d' % pj])
            S.dve(lambda e, P=P, half=half: e.tensor_tensor(
                out=tmp[i][:, half * 512:(half + 1) * 512], in0=P[:, :],
                in1=gt[:, half * 512:(half + 1) * 512], op=ALU.mult),
                r=['ps%d' % pj, mk], w=[('ztmp', i, half)])
        S.pool(lambda e: e.tensor_tensor(out=xr[i][:], in0=xr[i][:], in1=tmp[i][:], op=ALU.add),
               r=['zx%d' % i, ('ztmp', i, 0), ('ztmp', i, 1)], w=['zx%d' % i])
        S.dma(self.res_dst(t), xr[i][:], r=['zx%d' % i], w=[('res', t)])

    def _phase_z(self, les, wout_key, n_kc, uT_fn, tiles):
        with ExitStack() as tes:
            self._z_begin(tes)
            for t in tiles:
                uT, ukeys = uT_fn(t)
                self._z_tile(t, uT, ukeys, wout_key, n_kc)
            self.S.barrier()

    def _layer(self, les, l, first):
        self._first = first
        self.res_cur = lambda t: self.res_src(t, first)
        if not hasattr(self, 'eps_col'):
            self.eps_col = self.sb(self.es, "eps_col", [128, 1], F32)
            self.S.dve(lambda e: e.memset(self.eps_col[:], EPS), w=['eps'])
        self._phase_a(les, l, first)
        kind = l % 4
        if kind == 2:
            self._conv(les, l)
        elif kind == 3:
            self._hgrn(les, l)
        elif kind == 1:
            self._attn(les, l)
        elif kind == 0:
            self._mlstm(les, l)
        else:
            raise NotImplementedError


    def proj_tm(self, P, pkey, w, wkey, t, c0, n):
        hT = self.hT
        for k in range(KC):
            self.S.pe(lambda e, k=k: e.matmul(P, lhsT=hT[:, k, t * 128:(t + 1) * 128], rhs=w[:, k, c0:c0 + n],
                                              start=(k == 0), stop=(k == KC - 1)),
                      r=[wkey, ('hT', t)], w=[pkey])

    def proj_fm(self, P, pkey, w, wkey, t0, nt, c0, m=128):
        keys = [('hT', tt) for tt in range(t0 // 128, (t0 + nt + 127) // 128)]
        hT = self.hT
        for k in range(KC):
            self.S.pe(lambda e, k=k: e.matmul(P, lhsT=w[:, k, c0:c0 + m], rhs=hT[:, k, t0:t0 + nt],
                                              start=(k == 0), stop=(k == KC - 1)),
                      r=[wkey] + keys, w=[pkey])

    def _hgrn(self, les, l):
        S = self.S
        dr = self.dr
        PS = self.PS
        need_ctx = l < 3
        osc = [dr['osc0'], dr['osc1']]
        order = [list(range(NT)), [1, 0] + list(range(NT - 1, NT_CTX - 1, -1))]
        def group_body(g):
            with ExitStack() as ges:
                W = {}
                for nm, c0 in (('q', 0), ('f0', D), ('f1', 2 * D), ('i', 3 * D)):
                    W[nm] = self.sb(ges, "hgw_" + nm, [128, KC, 512], BF16)
                    self.load_w(W[nm][:], 'hgw_' + nm, dr['hg_w_in'], c0 + g * 512, 512)
                lbB = [self.sb(ges, "lbB%d" % d, [128, 512], F32) for d in range(2)]
                fbB = [self.sb(ges, "fbB%d" % d, [128, 512], F32) for d in range(2)]
                with ExitStack() as tes:
                    raw = self.sb(tes, "lbraw", [128, 4, 512], F32)
                    den = self.sb(tes, "lbden", [128, 512], F32)
                    for d in range(2):
                        for j in range(4):
                            S.dma(raw[:, j, :], dr['hg_lb'][d, j:j + 1, g * 512:(g + 1) * 512].partition_broadcast(128),
                                  w=['lbraw'])
                        S.dma(fbB[d][:], dr['hg_f_b'][d:d + 1, g * 512:(g + 1) * 512].partition_broadcast(128),
                              w=['fbB%d' % d])
                        S.act(lambda e: e.activation(out=raw[:], in_=raw[:], func=AF.Exp), r=['lbraw'], w=['lbraw'])
                        S.dve(lambda e: e.tensor_tensor(out=den[:], in0=raw[:, 0, :], in1=raw[:, 1, :], op=ALU.add),
                              r=['lbraw'], w=['lbden'])
                        S.dve(lambda e: e.tensor_tensor(out=den[:], in0=den[:], in1=raw[:, 2, :], op=ALU.add),
                              r=['lbraw', 'lbden'], w=['lbden'])
                        S.dve(lambda e: e.tensor_tensor(out=den[:], in0=den[:], in1=raw[:, 3, :], op=ALU.add),
                              r=['lbraw', 'lbden'], w=['lbden'])
                        S.dve(lambda e: e.reciprocal(out=den[:], in_=den[:]), r=['lbden'], w=['lbden'])
                        S.dve(lambda e, d=d: e.memset(lbB[d][:], 0.0), w=['lbB%d' % d])
                        for j in range(1, l + 1):
                            S.dve(lambda e, d=d, j=j: e.tensor_tensor(out=lbB[d][:], in0=lbB[d][:], in1=raw[:, j, :],
                                                                      op=ALU.add), r=['lbraw', 'lbB%d' % d],
                                  w=['lbB%d' % d])
                        S.dve(lambda e, d=d: e.tensor_tensor(out=lbB[d][:], in0=lbB[d][:], in1=den[:], op=ALU.mult),
                              r=['lbden', 'lbB%d' % d], w=['lbB%d' % d])
                    S.barrier()
                St = [[self.sb(ges, "hgS%d_%d" % (d, h), [128, 128], F32) for h in range(4)] for d in range(2)]
                Sb = [[self.sb(ges, "hgSb%d_%d" % (d, h), [128, 128], BF16) for h in range(4)] for d in range(2)]
                for d in range(2):
                    for h in range(4):
                        S.pool(lambda e, d=d, h=h: e.memset(St[d][h][:], 0.0), w=[('St', d, h)])
                        S.pool(lambda e, d=d, h=h: e.memset(Sb[d][h][:], 0.0), w=[('Sb', d, h)])
                def wt(nm, shape, dt):
                    return [self.sb(ges, "%s%d" % (nm, d), shape, dt) for d in range(2)]
                t0_ = wt("hg_t0", [128, 512], F32)
                sg_ = wt("hg_sg", [128, 512], F32)
                om_ = wt("hg_om", [128, 512], F32)
                lf_ = wt("hg_lf", [128, 512], F32)
                kk_ = wt("hg_kk", [128, 512], F32)
                er_ = wt("hg_er", [128, 512], F32)
                ke_ = wt("hg_ke", [128, 512], BF16)
                ii_ = wt("hg_ii", [128, 512], BF16)
                ot_ = wt("hg_ot", [128, 512], F32)
                qs_ = wt("hg_qs", [128, 512], F32)
                ea_ = wt("hg_ea", [128, 512], F32)
                ed_ = wt("hg_ed", [128, 512], F32)
                qe_ = wt("hg_qe", [128, 512], BF16)
                qd_ = wt("hg_qd", [128, 512], BF16)
                kdT_ = wt("hg_kdT", [128, 512], BF16)
                atm_ = wt("hg_atm", [128, 512], BF16)
                for step in range(NT):
                    for d in range(2):
                        a = order[d][step]
                        C = self.cfb[d]
                        last = 127 if d == 0 else 0
                        dk = lambda nm: ('hg', nm, d)
                        t0, sg, om, lf, kk, er, ke, ii, ot = (x[d] for x in (t0_, sg_, om_, lf_, kk_, er_, ke_, ii_, ot_))
                        qs, ea, ed, qe, qd, kdT, atm = (x[d] for x in (qs_, ea_, ed_, qe_, qd_, kdT_, atm_))
                        fw = W['f%d' % d]
                        self.proj_tm(PS[0][:, :], 'ps0', fw, 'hgw_f%d' % d, a, 0, 512)
                        S.dve(lambda e, t0=t0, d=d: e.tensor_tensor(out=t0[:], in0=PS[0][:, :], in1=fbB[d][:], op=ALU.add),
                              r=['ps0', 'fbB%d' % d], w=[dk('t0')])
                        S.act(lambda e, t0=t0, sg=sg: e.activation(out=sg[:], in_=t0[:], func=AF.Sigmoid),
                              r=[dk('t0')], w=[dk('sg')])
                        S.dve(lambda e, sg=sg, om=om: e.tensor_scalar(out=om[:], in0=sg[:], scalar1=-1.0, scalar2=1.0,
                                                                      op0=ALU.mult, op1=ALU.add),
                              r=[dk('sg')], w=[dk('om')])
                        S.pool(lambda e, om=om, d=d: e.tensor_tensor(out=om[:], in0=om[:], in1=lbB[d][:], op=ALU.mult),
                               r=[dk('om'), 'lbB%d' % d], w=[dk('om')])
                        S.dve(lambda e, om=om, sg=sg: e.tensor_tensor(out=sg[:], in0=om[:], in1=sg[:], op=ALU.add),
                              r=[dk('om'), dk('sg')], w=[dk('sg')])
                        S.act(lambda e, sg=sg, lf=lf: e.activation(out=lf[:], in_=sg[:], func=AF.Ln),
                              r=[dk('sg')], w=[dk('lf')])
                        S.pool(lambda e, sg=sg, kk=kk: e.tensor_scalar(out=kk[:], in0=sg[:], scalar1=-1.0, scalar2=1.0,
                                                                       op0=ALU.mult, op1=ALU.add),
                               r=[dk('sg')], w=[dk('kk')])
                        S.pe(lambda e, C=C, lf=lf: e.matmul(PS[1][:, :], lhsT=C[:, 128:256], rhs=lf[:], start=True, stop=True),
                             r=['cfb', dk('lf')], w=['ps1'])
                        S.act(lambda e, er=er: e.activation(out=er[:], in_=PS[1][:, :], func=AF.Exp),
                              r=['ps1'], w=[dk('er')])
                        S.dve(lambda e, kk=kk, er=er, ke=ke: e.tensor_tensor(out=ke[:], in0=kk[:], in1=er[:], op=ALU.mult),
                              r=[dk('kk'), dk('er')], w=[dk('ke')])
                        self.proj_tm(PS[0][:, :], 'ps0', W['i'], 'hgw_i', a, 0, 512)
                        S.act(lambda e, ii=ii: e.copy(out=ii[:], in_=PS[0][:, :]), r=['ps0'], w=[dk('ii')])
                        for hh in range(4):
                            self.proj_fm(PS[2][:, hh * 128:(hh + 1) * 128], 'ps2', W['q'], 'hgw_q', a * 128, 128, hh * 128)
                        S.act(lambda e, qs=qs: e.activation(out=qs[:], in_=PS[2][:, :], func=AF.Silu),
                              r=['ps2'], w=[dk('qs')])
                        for hh in range(4):
                            Pr = PS[3 + hh // 2]
                            S.pe(lambda e, Pr=Pr, hh=hh, lf=lf, C=C: e.matmul(
                                Pr[:, (hh % 2) * 256:(hh % 2) * 256 + 256], lhsT=lf[:, hh * 128:(hh + 1) * 128],
                                rhs=C[:, :], start=True, stop=True), r=[dk('lf'), 'cfb'], w=['ps%d' % (3 + hh // 2)])
                        for half in range(2):
                            Pr = PS[3 + half].rearrange("p (h two n) -> p h two n", two=2, n=128)
                            S.act(lambda e, Pr=Pr, ea=ea, half=half: e.activation(
                                out=ea[:, half * 256:(half + 1) * 256].rearrange("p (h n) -> p h n", n=128),
                                in_=Pr[:, :, 0, :], func=AF.Exp), r=['ps%d' % (3 + half)], w=[dk('ea')])
                            S.act(lambda e, Pr=Pr, ed=ed, half=half: e.activation(
                                out=ed[:, half * 256:(half + 1) * 256].rearrange("p (h n) -> p h n", n=128),
                                in_=Pr[:, :, 1, :], func=AF.Exp, scale=-1.0), r=['ps%d' % (3 + half)], w=[dk('ed')])
                        S.dve(lambda e, qs=qs, ea=ea, qe=qe: e.tensor_tensor(out=qe[:], in0=qs[:], in1=ea[:], op=ALU.mult),
                              r=[dk('qs'), dk('ea')], w=[dk('qe')])
                        S.pool(lambda e, qs=qs, ed=ed, qd=qd: e.tensor_tensor(out=qd[:], in0=qs[:], in1=ed[:], op=ALU.mult),
                               r=[dk('qs'), dk('ed')], w=[dk('qd')])
                        P5b = PS[5][:].bitcast(BF16)
                        for hh in range(4):
                            S.pe(lambda e, hh=hh, ke=ke: e.transpose(out=P5b[:, hh * 128:(hh + 1) * 128],
                                                                     in_=ke[:, hh * 128:(hh + 1) * 128],
                                                                     identity=self.ident_b[:]),
                                 r=[dk('ke'), 'ident_b'], w=['ps5'])
                        S.act(lambda e, kdT=kdT: e.copy(out=kdT[:], in_=P5b[:, 0:512]), r=['ps5'], w=[dk('kdT')])
                        for hh in range(4):
                            S.pe(lambda e, hh=hh, kdT=kdT, qd=qd: e.matmul(
                                PS[6][:, hh * 128:(hh + 1) * 128], lhsT=kdT[:, hh * 128:(hh + 1) * 128],
                                rhs=qd[:, hh * 128:(hh + 1) * 128], start=True, stop=True),
                                r=[dk('kdT'), dk('qd')], w=['ps6'])
                        S.dve(lambda e, atm=atm, C=C: e.tensor_tensor(
                            out=atm[:].rearrange("p (h n) -> p h n", n=128),
                            in0=PS[6][:].rearrange("p (h n) -> p h n", n=128),
                            in1=C[:, 0:128].unsqueeze(1).to_broadcast([128, 4, 128]), op=ALU.mult),
                            r=['ps6', 'cfb'], w=[dk('atm')])
                        for hh in range(4):
                            S.pe(lambda e, hh=hh, atm=atm, ii=ii: e.matmul(
                                PS[7][:, hh * 128:(hh + 1) * 128], lhsT=atm[:, hh * 128:(hh + 1) * 128],
                                rhs=ii[:, hh * 128:(hh + 1) * 128], start=True, stop=False),
                                r=[dk('atm'), dk('ii')], w=['ps7'])
                            S.pe(lambda e, hh=hh, qe=qe, d=d: e.matmul(
                                PS[7][:, hh * 128:(hh + 1) * 128], lhsT=qe[:, hh * 128:(hh + 1) * 128],
                                rhs=Sb[d][hh][:], start=False, stop=True),
                                r=[dk('qe'), ('Sb', d, hh)], w=['ps7'])
                        S.act(lambda e, ot=ot: e.copy(out=ot[:], in_=PS[7][:, :]), r=['ps7'], w=[dk('ot')])
                        S.dma(osc[d][a * 128:(a + 1) * 128, g * 512:(g + 1) * 512], ot[:], r=[dk('ot')],
                              w=[('osc', d, a, g)])
                        for hh in range(4):
                            S.pe(lambda e, hh=hh, ke=ke, ii=ii: e.matmul(
                                PS[1][:, hh * 128:(hh + 1) * 128], lhsT=ke[:, hh * 128:(hh + 1) * 128],
                                rhs=ii[:, hh * 128:(hh + 1) * 128], start=True, stop=True),
                                r=[dk('ke'), dk('ii')], w=['ps1'])
                        for hh in range(4):
                            S.dve(lambda e, hh=hh, d=d, ea=ea, last=last: e.scalar_tensor_tensor(
                                out=St[d][hh][:], in0=St[d][hh][:], scalar=ea[:, hh * 128 + last:hh * 128 + last + 1],
                                in1=PS[1][:, hh * 128:(hh + 1) * 128], op0=ALU.mult, op1=ALU.add),
                                r=[('St', d, hh), dk('ea'), 'ps1'], w=[('St', d, hh)])
                            S.pool(lambda e, hh=hh, d=d: e.tensor_copy(out=Sb[d][hh][:], in_=St[d][hh][:]),
                                   r=[('St', d, hh)], w=[('Sb', d, hh)])
                S.barrier()
        for g_ in range(2):
            group_body(g_)
        tiles = list(range(NT)) if need_ctx else list(range(NT_CTX, NT))
        uT = self.sb(les, "uT", [128, KC, T_ALL], BF16)
        self.wout = self.sb(les, "wout", [128, KC, D], BF16)
        with ExitStack() as tes:
            wz = self.sb(tes, "hgw_z", [128, KC, D], BF16)
            self.load_w(wz[:], 'hgw_z', dr['hg_w_in'], 4 * D, D)
            self.load_w(self.wout[:], 'wout', dr['hg_w_out'], 0, D)
            hg = self.sb(tes, "hg_g", [128, D], F32)
            S.dma(hg[:], dr['hg_head_g'][0:1, :].partition_broadcast(128), w=['hg_g'])
            o0 = [self.sb(tes, "fo0_%d" % i, [128, D], F32) for i in range(2)]
            o1 = [self.sb(tes, "fo1_%d" % i, [128, D], F32) for i in range(2)]
            sq = self.sb(tes, "fsq", [128, D], F32)
            ssq = self.sb(tes, "fssq", [128, 2 * 8 * NT], F32)
            szt = self.sb(tes, "fsz", [128, D], F32)
            ub = [self.sb(tes, "fub%d" % i, [128, D], BF16) for i in range(2)]
            for n_, t in enumerate(tiles):
                i = n_ % 2
                S.dma(o0[i][:], osc[0][t * 128:(t + 1) * 128, :], r=[('osc', 0, t, 0), ('osc', 0, t, 1)], w=['fo0_%d' % i])
                S.dma(o1[i][:], osc[1][t * 128:(t + 1) * 128, :], r=[('osc', 1, t, 0), ('osc', 1, t, 1)], w=['fo1_%d' % i])
                S.dve(lambda e, i=i: e.tensor_tensor(out=o0[i][:], in0=o0[i][:], in1=o1[i][:], op=ALU.add),
                      r=['fo0_%d' % i, 'fo1_%d' % i], w=['fo0_%d' % i])
                S.pool(lambda e, i=i: e.tensor_tensor(out=sq[:], in0=o0[i][:], in1=o0[i][:], op=ALU.mult),
                       r=['fo0_%d' % i], w=['fsq'])
                c0 = 16 * n_
                S.dve(lambda e, c0=c0: e.tensor_reduce(out=ssq[:, c0:c0 + 8], in_=sq[:].rearrange("p (h n) -> p h n", n=128),
                                                       axis=AX.X, op=ALU.add), r=['fsq'], w=[('fssq', n_)])
                S.act(lambda e, c0=c0: e.activation(out=ssq[:, c0 + 8:c0 + 16], in_=ssq[:, c0:c0 + 8], func=AF.Sqrt,
                                                    scale=1.0 / 128, bias=self.eps_col[:, 0:1]),
                      r=[('fssq', n_), 'eps'], w=[('fssq1', n_)])
                S.dve(lambda e, c0=c0: e.reciprocal(out=ssq[:, c0:c0 + 8], in_=ssq[:, c0 + 8:c0 + 16]),
                      r=[('fssq1', n_)], w=[('fssq', n_)])
                S.dve(lambda e, i=i, c0=c0: e.tensor_tensor(
                    out=o0[i][:].rearrange("p (h n) -> p h n", n=128), in0=o0[i][:].rearrange("p (h n) -> p h n", n=128),
                    in1=ssq[:, c0:c0 + 8].unsqueeze(2).to_broadcast([128, 8, 128]), op=ALU.mult),
                    r=['fo0_%d' % i, ('fssq', n_)], w=['fo0_%d' % i])
                S.pool(lambda e, i=i: e.tensor_tensor(out=o0[i][:], in0=o0[i][:], in1=hg[:], op=ALU.mult),
                       r=['fo0_%d' % i, 'hg_g'], w=['fo0_%d' % i])
                for half in range(2):
                    self.proj_tm(PS[half][:, :], 'ps%d' % half, wz, 'hgw_z', t, half * 512, 512)
                    S.act(lambda e, half=half: e.activation(out=szt[:, half * 512:(half + 1) * 512], in_=PS[half][:, :],
                                                            func=AF.Silu), r=['ps%d' % half], w=[('fsz', half)])
                S.dve(lambda e, i=i: e.tensor_tensor(out=ub[i][:], in0=o0[i][:], in1=szt[:], op=ALU.mult),
                      r=['fo0_%d' % i, ('fsz', 0), ('fsz', 1)], w=['fub%d' % i])
                pj = 2 + i
                Pb = PS[pj][:].bitcast(BF16)
                for k in range(KC):
                    S.pe(lambda e, Pb=Pb, i=i, k=k: e.transpose(out=Pb[:, k * 128:(k + 1) * 128],
                                                                in_=ub[i][:, k * 128:(k + 1) * 128],
                                                                identity=self.ident_b[:]),
                         r=['fub%d' % i, 'ident_b'], w=['ps%d' % pj])
                S.act(lambda e, Pb=Pb, t=t: e.copy(out=uT[:, :, t * 128:(t + 1) * 128],
                                                   in_=Pb.rearrange("p (c n) -> p c n", c=KC)),
                      r=['ps%d' % pj], w=[('uT', t)])
            S.barrier()

        def uT_fn(t):
            return uT[:, :, t * 128:(t + 1) * 128], [('uT', t)]
        self._phase_z(les, 'wout', KC, uT_fn, tiles)


    def _attn(self, les, l):
        S = self.S
        dr = self.dr
        PS = self.PS
        need_ctx = l < 3
        QT = self.sb(les, "QT", [128, 8, T_ALL], BF16)
        KT = self.sb(les, "KT", [128, 4, T_ALL], BF16)
        Va = self.sb(les, "Vaug", [128, NT, 4 * 72], BF16)
        esink = self.sb(les, "esink", [128, 16], F32)
        maskP = self.sb(les, "maskP", [128, 128], BF16)
        maskN = self.sb(les, "maskN", [128, 128], BF16)
        S.dma(esink[:], dr['at_sink'][0:1, :].partition_broadcast(128), w=['esink'])
        S.act(lambda e: e.activation(out=esink[:], in_=esink[:], func=AF.Exp), r=['esink'], w=['esink'])
        S.dve(lambda e: e.tensor_scalar_mul(out=maskP[:], in0=self.cfb[1][:, 128:256], scalar1=-30000.0),
              r=['cfb'], w=['maskP'])
        S.dve(lambda e: e.tensor_scalar_mul(out=maskN[:], in0=self.cfb[0][:, 128:256], scalar1=-30000.0),
              r=['cfb'], w=['maskN'])
        S.pool(lambda e: e.memset(Va[:], 1.0), w=['Va_init'])
        with ExitStack() as tes:
            wqk = self.sb(tes, "wqk", [128, KC, 1536], BF16)
            self.load_w(wqk[:], 'wqk', dr['at_w_in'], 0, 1536)
            G = self.sb(tes, "qkG", [128, 1280], F32)
            for h in range(16):
                S.dma(G[:, h * 64:(h + 1) * 64], dr['at_q_g'][0:1, :].partition_broadcast(128), w=['qkG'])
            for j in range(4):
                S.dma(G[:, 1024 + j * 64:1024 + (j + 1) * 64], dr['at_k_g'][0:1, :].partition_broadcast(128), w=['qkG'])
            S.dve(lambda e: e.tensor_scalar_mul(out=G[:, 0:1024], in0=G[:, 0:1024], scalar1=0.125), r=['qkG'], w=['qkG'])
            qk = self.sb(tes, "qk", [128, 1280], F32)
            sq = self.sb(tes, "qksq", [128, 1280], F32)
            ssq = self.sb(tes, "qkss", [128, 40 * NT], F32)
            qkr = self.sb(tes, "qkr", [128, 1280], BF16)
            kdup = self.sb(tes, "kdup", [128, 512], BF16)
            tt = [self.sb(tes, "rt%d" % i, [128, 640], F32) for i in range(4)]
            cs = [self.sb(tes, "cs%d" % i, [128, 64], F32) for i in range(2)]
            for t in range(NT):
                self.proj_tm(PS[0][:, :], 'ps0', wqk, 'wqk', t, 0, 512)
                self.proj_tm(PS[1][:, :], 'ps1', wqk, 'wqk', t, 512, 512)
                self.proj_tm(PS[2][:, :], 'ps2', wqk, 'wqk', t, 1024, 512)
                S.act(lambda e: e.copy(out=qk[:, 0:512], in_=PS[0][:, :]), r=['ps0'], w=[('qk', 0)])
                S.act(lambda e: e.copy(out=qk[:, 512:1024], in_=PS[1][:, :]), r=['ps1'], w=[('qk', 1)])
                S.act(lambda e: e.copy(out=qk[:, 1024:1280], in_=PS[2][:, 0:256]), r=['ps2'], w=[('qk', 2)])
                S.act(lambda e, t=t: e.copy(out=Va[:, t, :].rearrange("p (j c) -> p j c", c=72)[:, :, 0:64],
                                            in_=PS[2][:, 256:512].rearrange("p (j c) -> p j c", c=64)),
                      r=['ps2', 'Va_init'], w=[('Va', t)])
                qka = [('qk', 0), ('qk', 1), ('qk', 2)]
                import os
                dbg = os.environ.get('KDBG', '')
                if dbg == 'prepA':
                    continue
                S.pool(lambda e: e.tensor_tensor(out=sq[:], in0=qk[:], in1=qk[:], op=ALU.mult), r=qka, w=['qksq'])
                c0 = 40 * t
                S.dve(lambda e, c0=c0: e.tensor_reduce(out=ssq[:, c0:c0 + 20], in_=sq[:].rearrange("p (h n) -> p h n", n=64),
                                                       axis=AX.X, op=ALU.add), r=['qksq'], w=[('qkss', t)])
                S.act(lambda e, c0=c0: e.activation(out=ssq[:, c0 + 20:c0 + 40], in_=ssq[:, c0:c0 + 20], func=AF.Sqrt,
                                                    scale=1.0 / 64, bias=self.eps_col[:, 0:1]),
                      r=[('qkss', t), 'eps'], w=[('qkss1', t)])
                S.dve(lambda e, c0=c0: e.reciprocal(out=ssq[:, c0:c0 + 20], in_=ssq[:, c0 + 20:c0 + 40]),
                      r=[('qkss1', t)], w=[('qkss', t)])
                S.dve(lambda e, c0=c0: e.tensor_tensor(
                    out=qk[:].rearrange("p (h n) -> p h n", n=64), in0=qk[:].rearrange("p (h n) -> p h n", n=64),
                    in1=ssq[:, c0:c0 + 20].unsqueeze(2).to_broadcast([128, 20, 64]), op=ALU.mult),
                    r=qka + [('qkss', t)], w=qka)
                if dbg == 'prepB':
                    continue
                if t < NT_CTX:
                    S.pool(lambda e: e.tensor_tensor(out=qkr[:], in0=qk[:], in1=G[:], op=ALU.mult),
                           r=qka + ['qkG'], w=['qkr'])
                else:
                    S.pool(lambda e: e.tensor_tensor(out=qk[:], in0=qk[:], in1=G[:], op=ALU.mult),
                           r=qka + ['qkG'], w=qka)
                    c = cs[t % 2]
                    ck = 'cs%d' % (t % 2)
                    tl = t - NT_CTX
                    S.dma(c[:], dr['rope'][tl * 128:(tl + 1) * 128, :], w=[ck])
                    v4 = qk[:].rearrange("p (h two n) -> p h two n", two=2, n=32)
                    x1, x2 = v4[:, :, 0, :], v4[:, :, 1, :]
                    r4 = qkr[:].rearrange("p (h two n) -> p h two n", two=2, n=32)
                    cosb = c[:, 0:32].unsqueeze(1).to_broadcast([128, 20, 32])
                    sinb = c[:, 32:64].unsqueeze(1).to_broadcast([128, 20, 32])
                    tv = [x[:].rearrange("p (h n) -> p h n", n=32) for x in tt]
                    S.dve(lambda e, x1=x1, cosb=cosb, tv=tv: e.tensor_tensor(out=tv[0], in0=x1, in1=cosb, op=ALU.mult),
                          r=qka + [ck], w=['rt0'])
                    S.dve(lambda e, x2=x2, sinb=sinb, tv=tv: e.tensor_tensor(out=tv[1], in0=x2, in1=sinb, op=ALU.mult),
                          r=qka + [ck], w=['rt1'])
                    S.dve(lambda e, x1=x1, sinb=sinb, tv=tv: e.tensor_tensor(out=tv[2], in0=x1, in1=sinb, op=ALU.mult),
                          r=qka + [ck], w=['rt2'])
                    S.dve(lambda e, x2=x2, cosb=cosb, tv=tv: e.tensor_tensor(out=tv[3], in0=x2, in1=cosb, op=ALU.mult),
                          r=qka + [ck], w=['rt3'])
                    S.dve(lambda e, r4=r4, tv=tv: e.tensor_tensor(out=r4[:, :, 0, :], in0=tv[0], in1=tv[1], op=ALU.subtract),
                          r=['rt0', 'rt1'], w=[('qkr', 0)])
                    S.pool(lambda e, r4=r4, tv=tv: e.tensor_tensor(out=r4[:, :, 1, :], in0=tv[2], in1=tv[3], op=ALU.add),
                           r=['rt2', 'rt3'], w=[('qkr', 1)])
                qrk = ['qkr', ('qkr', 0), ('qkr', 1)]
                if dbg == 'prepC':
                    continue
                for dd in range(2):
                    S.pool(lambda e, dd=dd: e.tensor_copy(
                        out=kdup[:].rearrange("p (j two n) -> p j two n", two=2, n=64)[:, :, dd, :],
                        in_=qkr[:, 1024:1280].rearrange("p (j n) -> p j n", n=64)),
                        r=qrk, w=['kdup'])
                P3b = PS[3][:].bitcast(BF16)
                P4b = PS[4][:].bitcast(BF16)
                for b in range(8):
                    S.pe(lambda e, b=b: e.transpose(out=P3b[:, b * 128:(b + 1) * 128], in_=qkr[:, b * 128:(b + 1) * 128],
                                                    identity=self.ident_b[:]), r=qrk + ['ident_b'], w=['ps3'])
                for j in range(4):
                    S.pe(lambda e, j=j: e.transpose(out=P4b[:, j * 128:(j + 1) * 128], in_=kdup[:, j * 128:(j + 1) * 128],
                                                    identity=self.ident_b[:]), r=['kdup', 'ident_b'], w=['ps4'])
                S.act(lambda e, t=t: e.copy(out=QT[:, :, t * 128:(t + 1) * 128],
                                            in_=P3b.rearrange("p (c n) -> p c n", n=128)), r=['ps3'], w=[('QT', t)])
                S.act(lambda e, t=t: e.copy(out=KT[:, :, t * 128:(t + 1) * 128],
                                            in_=P4b[:, 0:512].rearrange("p (c n) -> p c n", n=128)), r=['ps4'], w=[('KT', t)])
            S.barrier()
        import os
        if os.environ.get('KDBG', '').startswith('prep'):
            return
        with ExitStack() as tes:
            wz = self.sb(tes, "at_wz", [128, KC, D], BF16)
            self.wout = self.sb(tes, "wout", [128, KC, D], BF16)
            self.load_w(wz[:], 'at_wz', dr['at_w_in'], 1536, D)
            self.load_w(self.wout[:], 'wout', dr['at_w_out'], 0, D)
            PT = [self.sb(tes, "PT%d" % i, [128, 512], BF16) for i in range(10)]
            oat = self.sb(tes, "oat", [128, D], F32)
            den = self.sb(tes, "aden", [128, 16], F32)
            szt = self.sb(tes, "asz", [128, D], F32)
            ub = self.sb(tes, "aub", [128, D], BF16)
            uTt = [self.sb(tes, "auT%d" % i, [128, KC, 128], BF16) for i in range(2)]
            self._z_begin(tes)
            tiles = list(range(NT)) if need_ctx else list(range(NT_CTX, NT))
            nst = 0
            dbg = os.environ.get('KDBG', '')
            for n_, t in enumerate(tiles):
                if t < NT_CTX:
                    keys = [(0, None), (1, None)]
                else:
                    keys = []
                    if t - 1 >= NT_CTX:
                        keys.append((t - 1, 'P'))
                    keys.append((t, None))
                    if t + 1 < NT:
                        keys.append((t + 1, 'N'))
                    keys += [(0, None), (1, None)]
                for j in range(4):
                    Po = PS[2 + j]
                    pok = 'ps%d' % (2 + j)
                    pts = []
                    for ki, (s_, mk) in enumerate(keys):
                        b = nst % 2
                        pi = nst % 10
                        nst += 1
                        pts.append(pi)
                        PsAB = (PS[b], PS[6 + b])
                        pskAB = ('ps%d' % b, 'ps%d' % (6 + b))
                        if os.environ.get('KNOMASK'):
                            mk = None
                        if mk is not None:
                            mt = maskP if mk == 'P' else maskN
                            for ab in range(2):
                                S.pe(lambda e, ab=ab, mt=mt, PsAB=PsAB: e.matmul(
                                    PsAB[ab][:, 0:256], lhsT=self.ident_b[:],
                                    rhs=mt[:].unsqueeze(1).to_broadcast([128, 2, 128]),
                                    start=True, stop=False), r=['ident_b', 'maskP', 'maskN'], w=[pskAB[ab]])
                        for gq in range(4):
                            h = 4 * j + gq
                            pb = (h % 2) * 64
                            ab = h % 2
                            S.pe(lambda e, PsAB=PsAB, ab=ab, gq=gq, h=h, pb=pb, s_=s_, mk=mk, j=j, t=t: e.matmul(
                                PsAB[ab][:, (gq // 2) * 128:(gq // 2 + 1) * 128],
                                lhsT=KT[pb:pb + 64, j, s_ * 128:(s_ + 1) * 128],
                                rhs=QT[pb:pb + 64, h // 2, t * 128:(t + 1) * 128],
                                start=(mk is None), stop=(mk is None or gq >= 2)),
                                r=[('KT', s_), ('QT', t)], w=[pskAB[ab]])
                        for ab in range(2):
                            S.act(lambda e, ab=ab, PsAB=PsAB, pi=pi: e.activation(
                                out=PT[pi][:, ab * 256:(ab + 1) * 256], in_=PsAB[ab][:, 0:256], func=AF.Exp),
                                r=[pskAB[ab]], w=[('PT', pi, ab)])
                    if dbg == 'attS':
                        continue
                    for gq in range(4):
                        for ki, (s_, mk) in enumerate(keys):
                            pi = pts[ki]
                            S.pe(lambda e, Po=Po, gq=gq, pi=pi, s_=s_, j=j, ki=ki, nk=len(keys): e.matmul(
                                Po[:, gq * 65:(gq + 1) * 65],
                                lhsT=PT[pi][:, (gq % 2) * 256 + (gq // 2) * 128:(gq % 2) * 256 + (gq // 2) * 128 + 128],
                                rhs=Va[:, s_, j * 72:j * 72 + 65], start=(ki == 0), stop=(ki == nk - 1)),
                                r=[('PT', pi, gq % 2), ('Va', s_)], w=[pok])
                    Pv = Po[:, 0:260].rearrange("p (g c) -> p g c", c=65)
                    S.dve(lambda e, Pv=Pv, j=j: e.tensor_tensor(out=den[:, 4 * j:4 * j + 4], in0=Pv[:, :, 64],
                                                                in1=esink[:, 4 * j:4 * j + 4], op=ALU.add),
                          r=[pok, 'esink'], w=[('aden', j)])
                    S.dve(lambda e, j=j: e.reciprocal(out=den[:, 4 * j:4 * j + 4], in_=den[:, 4 * j:4 * j + 4]),
                          r=[('aden', j)], w=[('aden', j)])
                    S.dve(lambda e, Pv=Pv, j=j: e.tensor_tensor(
                        out=oat[:, j * 256:(j + 1) * 256].rearrange("p (g c) -> p g c", c=64), in0=Pv[:, :, 0:64],
                        in1=den[:, 4 * j:4 * j + 4].unsqueeze(2).to_broadcast([128, 4, 64]), op=ALU.mult),
                        r=[pok, ('aden', j)], w=[('oat', j)])
                if dbg in ('attS', 'attPV'):
                    continue
                for half in range(2):
                    self.proj_tm(PS[6 + half][:, :], 'ps%d' % (6 + half), wz, 'at_wz', t, half * 512, 512)
                    S.act(lambda e, half=half: e.activation(out=szt[:, half * 512:(half + 1) * 512], in_=PS[6 + half][:, :],
                                                            func=AF.Silu), r=['ps%d' % (6 + half)], w=[('asz', half)])
                S.pool(lambda e: e.tensor_tensor(out=ub[:], in0=oat[:], in1=szt[:], op=ALU.mult),
                       r=[('oat', j) for j in range(4)] + [('asz', 0), ('asz', 1)], w=['aub'])
                P0b = PS[n_ % 2][:].bitcast(BF16)
                pk0 = 'ps%d' % (n_ % 2)
                for k in range(KC):
                    S.pe(lambda e, P0b=P0b, k=k: e.transpose(out=P0b[:, k * 128:(k + 1) * 128],
                                                             in_=ub[:, k * 128:(k + 1) * 128], identity=self.ident_b[:]),
                         r=['aub', 'ident_b'], w=[pk0])
                ut = uTt[n_ % 2]
                utk = 'auT%d' % (n_ % 2)
                S.act(lambda e, P0b=P0b, ut=ut: e.copy(out=ut[:], in_=P0b.rearrange("p (c n) -> p c n", n=128)),
                      r=[pk0], w=[utk])
                self._z_tile(t, ut, [utk], 'wout', KC)
            S.barrier()


    def _mlstm(self, les, l):
        S = self.S
        dr = self.dr
        PS = self.PS
        need_ctx = l < 3
        order = [list(range(NT)), [1, 0] + list(range(NT - 1, NT_CTX - 1, -1))]
        Gall = self.sb(les, "mlG", [128, NT * 16], F32)
        LF = self.sb(les, "mlLF", [128, NT * 16], F32)
        Bc = self.sb(les, "mlBc", [128, NT * 8], F32)
        Vc = self.sb(les, "mlVc", [128, NT * 8], F32)
        hgB = self.sb(les, "mlhg", [128, 2048], F32)
        S.dma(hgB[:], dr['ml_head_g'][0:1, :].partition_broadcast(128), w=['mlhg'])
        with ExitStack() as tes:
            wg = self.sb(tes, "mlwg", [128, KC, 16], BF16)
            gb = self.sb(tes, "mlgb", [128, 16], F32)
            self.load_w(wg[:], 'mlwg', dr['ml_w_in'], 8192, 16)
            S.dma(gb[:], dr['ml_gate_b'][0:1, :].partition_broadcast(128), w=['mlgb'])
            for t in range(NT):
                self.proj_tm(PS[0][:, t * 16:(t + 1) * 16], 'ps0', wg, 'mlwg', t, 0, 16)
            S.dve(lambda e: e.tensor_tensor(out=Gall[:].rearrange("p (t g) -> p t g", g=16),
                                            in0=PS[0][:, 0:NT * 16].rearrange("p (t g) -> p t g", g=16),
                                            in1=gb[:].unsqueeze(1).to_broadcast([128, NT, 16]), op=ALU.add),
                  r=['ps0', 'mlgb'], w=['mlG'])
            S.act(lambda e: e.activation(out=LF[:], in_=Gall[:], func=AF.Exp, scale=-1.0), r=['mlG'], w=['mlLF'])
            S.act(lambda e: e.activation(out=LF[:], in_=LF[:], func=AF.Ln, bias=self.ones_f[:, 0:1]),
                  r=['mlLF', 'ones_f'], w=['mlLF'])
            S.dve(lambda e: e.tensor_scalar_mul(out=LF[:], in0=LF[:], scalar1=-1.0), r=['mlLF'], w=['mlLF'])
            for t in range(NT):
                for d in range(2):
                    c0 = t * 16 + (2 * d + 1) * 4
                    S.pe(lambda e, t=t, d=d, c0=c0: e.matmul(PS[1][:, (t * 2 + d) * 4:(t * 2 + d) * 4 + 4],
                                                             lhsT=self.cfb[d][:, 0:128], rhs=LF[:, c0:c0 + 4],
                                                             start=True, stop=True), r=['cfb', 'mlLF'], w=['ps1'])
            S.act(lambda e: e.copy(out=Bc[:], in_=PS[1][:, 0:NT * 8]), r=['ps1'], w=['mlBc'])
            G4 = Gall[:].rearrange("p (t j h) -> p t j h", j=4, h=4)
            for d in range(2):
                S.dve(lambda e, d=d: e.tensor_tensor(
                    out=Vc[:].rearrange("p (t d h) -> p t d h", d=2, h=4)[:, :, d, :], in0=G4[:, :, 2 * d, :],
                    in1=Bc[:].rearrange("p (t d h) -> p t d h", d=2, h=4)[:, :, d, :], op=ALU.subtract),
                    r=['mlG', 'mlBc'], w=['mlVc'])
            S.act(lambda e: e.activation(out=Vc[:], in_=Vc[:], func=AF.Exp), r=['mlVc'], w=['mlVc'])
            S.barrier()
        def head_body(h):
            with ExitStack() as hes:
                qT = self.sb(hes, "mlqT", [128, 2, T_ALL], BF16)
                kT = self.sb(hes, "mlkT", [128, 2, T_ALL], BF16)
                Kt = self.sb(hes, "mlKt", [128, NT, 256], BF16)
                Vt = self.sb(hes, "mlVt", [128, NT, 520], BF16)
                Hs = self.sb(hes, "mlHs", [128, NT, 512], F32)
                S.pool(lambda e: e.memset(Hs[:], 0.0), w=[('Hs', t) for t in range(NT)])
                S.pool(lambda e: e.memset(Vt[:], 1.0), w=['Vt_init'])
                with ExitStack() as ses:
                    wq = self.sb(ses, "mlwq", [128, KC, 256], BF16)
                    wk = self.sb(ses, "mlwk", [128, KC, 256], BF16)
                    wv = self.sb(ses, "mlwv", [128, KC, 512], BF16)
                    self.load_w(wq[:], 'mlwq', dr['ml_w_in'], h * 256, 256)
                    self.load_w(wk[:], 'mlwk', dr['ml_w_in'], 1024 + h * 256, 256)
                    self.load_w(wv[:], 'mlwv', dr['ml_w_in'], 2048 + h * 512, 512)
                    blocks = [(0, T_CTX)] + [(T_CTX + i * 512, 512) for i in range(4)]
                    nb = 0
                    for (t0, n) in blocks:
                        for c in range(2):
                            for (w_, wk_, dst, sc) in ((wq, 'mlwq', qT, 1.0 / 16), (wk, 'mlwk', kT, 1.0)):
                                pj = nb % 2
                                nb += 1
                                self.proj_fm(PS[pj][:, 0:n], 'ps%d' % pj, w_, wk_, t0, n, c * 128)
                                S.act(lambda e, pj=pj, dst=dst, c=c, t0=t0, n=n, sc=sc: e.mul(
                                    out=dst[:, c, t0:t0 + n], in_=PS[pj][:, 0:n], mul=sc),
                                    r=['ps%d' % pj], w=[('qkT', id(dst), c, t0)])
                    for t in range(NT):
                        self.proj_tm(PS[2][:, 0:256], 'ps2', wk, 'mlwk', t, 0, 256)
                        S.act(lambda e, t=t: e.copy(out=Kt[:, t, :], in_=PS[2][:, 0:256]), r=['ps2'], w=[('Kt', t)])
                        self.proj_tm(PS[3][:, :], 'ps3', wv, 'mlwv', t, 0, 512)
                        S.act(lambda e, t=t: e.copy(out=Vt[:, t, 0:512], in_=PS[3][:, :]), r=['ps3', 'Vt_init'],
                              w=[('Vt', t)])
                    qk_keys = [('qkT', id(dst), c, t0) for dst in (qT, kT) for c in range(2) for (t0, n) in blocks]
                    Cst = [self.sb(ses, "mlC%d" % d, [128, 2, 512], F32) for d in range(2)]
                    Cb = [self.sb(ses, "mlCb%d" % d, [128, 2, 512], BF16) for d in range(2)]
                    Nst = [self.sb(ses, "mlN%d" % d, [128, 2], F32) for d in range(2)]
                    Nb = [self.sb(ses, "mlNb%d" % d, [128, 2], BF16) for d in range(2)]
                    for d in range(2):
                        S.pool(lambda e, d=d: e.memset(Cst[d][:], 0.0), w=[('C', d)])
                        S.pool(lambda e, d=d: e.memset(Cb[d][:], 0.0), w=[('Cb', d)])
                        S.pool(lambda e, d=d: e.memset(Nst[d][:], 0.0), w=[('N', d)])
                        S.pool(lambda e, d=d: e.memset(Nb[d][:], 0.0), w=[('Nb', d)])

                    def wt(nm, shape, dt):
                        return [self.sb(ses, "%s%d" % (nm, d), shape, dt) for d in range(2)]
                    ebt_ = wt("mlebt", [128, 128], F32)
                    em_ = wt("mlem", [128, 128], F32)
                    wcol_ = wt("mlw", [128, 4], F32)
                    pt_ = wt("mlpt", [128, 128], BF16)
                    qs_ = wt("mlqs", [128, 2, 128], BF16)
                    kw_ = wt("mlkw", [128, 256], BF16)
                    for step in range(NT):
                        for d in range(2):
                            a = order[d][step]
                            Tri = self.cfb[d][:, 0:128]
                            last = 127 if d == 0 else 0
                            dk = lambda nm: ('ml', nm, d)
                            ebt, em, wcol, pt, qs, kw = (x[d] for x in (ebt_, em_, wcol_, pt_, qs_, kw_))
                            lfc = a * 16 + (2 * d + 1) * 4 + h
                            vcol = Vc[:, a * 8 + d * 4 + h:a * 8 + d * 4 + h + 1]
                            P0, P1, P2, P3 = PS[0], PS[1], PS[2], PS[3]
                            S.pe(lambda e, lfc=lfc, Tri=Tri: e.matmul(P0[:, 0:128], lhsT=LF[:, lfc:lfc + 1].to_broadcast([128, 128]),
                                                                      rhs=Tri, start=True, stop=True),
                                 r=['mlLF', 'cfb'], w=['ps0'])
                            S.act(lambda e, ebt=ebt: e.activation(out=ebt[:], in_=P0[:, 0:128], func=AF.Exp),
                                  r=['ps0'], w=[dk('ebt')])
                            S.pool(lambda e, ebt=ebt, em=em, Tri=Tri: e.tensor_tensor(out=em[:], in0=ebt[:], in1=Tri, op=ALU.mult),
                                   r=[dk('ebt'), 'cfb'], w=[dk('em')])
                            S.dve(lambda e, wcol=wcol, ebt=ebt, vcol=vcol, last=last: e.tensor_tensor(
                                out=wcol[:, 0:1], in0=vcol, in1=ebt[:, last:last + 1], op=ALU.mult),
                                r=['mlVc', dk('ebt')], w=[dk('w')])
                            for c in range(2):
                                S.pe(lambda e, c=c, a=a: e.matmul(P1[:, 0:128], lhsT=kT[:, c, a * 128:(a + 1) * 128],
                                                                  rhs=qT[:, c, a * 128:(a + 1) * 128],
                                                                  start=(c == 0), stop=(c == 1)), r=qk_keys, w=['ps1'])
                            S.dve(lambda e, pt=pt, em=em, vcol=vcol: e.scalar_tensor_tensor(
                                out=pt[:], in0=P1[:, 0:128], scalar=vcol, in1=em[:], op0=ALU.mult, op1=ALU.mult),
                                r=['ps1', 'mlVc', dk('em')], w=[dk('pt')])
                            S.dve(lambda e, qs=qs, ebt=ebt, a=a: e.tensor_tensor(
                                out=qs[:], in0=qT[:, :, a * 128:(a + 1) * 128],
                                in1=ebt[:].unsqueeze(1).to_broadcast([128, 2, 128]), op=ALU.mult),
                                r=qk_keys + [dk('ebt')], w=[dk('qs')])
                            S.pe(lambda e, pt=pt, a=a: e.matmul(P2[:, :], lhsT=pt[:], rhs=Vt[:, a, 0:512], start=True, stop=False),
                                 r=[dk('pt'), ('Vt', a)], w=['ps2'])
                            for c in range(2):
                                S.pe(lambda e, qs=qs, c=c, d=d: e.matmul(P2[:, :], lhsT=qs[:, c, :], rhs=Cb[d][:, c, :],
                                                                         start=False, stop=(c == 1)),
                                     r=[dk('qs'), ('Cb', d)], w=['ps2'])
                            S.pe(lambda e, pt=pt, a=a: e.matmul(P3[:, 0:1], lhsT=pt[:], rhs=Vt[:, a, 512:513], start=True, stop=False),
                                 r=[dk('pt'), ('Vt', a)], w=['ps3'])
                            for c in range(2):
                                S.pe(lambda e, qs=qs, c=c, d=d: e.matmul(P3[:, 0:1], lhsT=qs[:, c, :], rhs=Nb[d][:, c:c + 1],
                                                                         start=False, stop=(c == 1)),
                                     r=[dk('qs'), ('Nb', d)], w=['ps3'])
                            S.dve(lambda e, wcol=wcol: e.tensor_copy(out=wcol[:, 3:4], in_=P3[:, 0:1]),
                                  r=['ps3'], w=[dk('dn')])
                            S.dve(lambda e, wcol=wcol: e.scalar_tensor_tensor(out=wcol[:, 1:2], in0=wcol[:, 3:4], scalar=-1.0,
                                                                              in1=wcol[:, 3:4], op0=ALU.mult, op1=ALU.max),
                                  r=[dk('dn')], w=[dk('rd')])
                            S.dve(lambda e, wcol=wcol: e.tensor_scalar_max(out=wcol[:, 1:2], in0=wcol[:, 1:2], scalar1=1.0),
                                  r=[dk('rd')], w=[dk('rd')])
                            S.dve(lambda e, wcol=wcol: e.reciprocal(out=wcol[:, 2:3], in_=wcol[:, 1:2]),
                                  r=[dk('rd')], w=[dk('rd2')])
                            S.dve(lambda e, wcol=wcol, a=a: e.scalar_tensor_tensor(
                                out=Hs[:, a, :], in0=P2[:, :], scalar=wcol[:, 2:3], in1=Hs[:, a, :], op0=ALU.mult, op1=ALU.add),
                                r=['ps2', dk('rd2'), ('Hs', a)], w=[('Hs', a)])
                            S.act(lambda e, kw=kw, wcol=wcol, a=a: e.mul(out=kw[:], in_=Kt[:, a, :], mul=wcol[:, 0:1]),
                                  r=[('Kt', a), dk('w')], w=[dk('kw')])
                            for c in range(2):
                                S.pe(lambda e, kw=kw, c=c, a=a: e.matmul(PS[4 + c][:, :], lhsT=kw[:, c * 128:(c + 1) * 128],
                                                                         rhs=Vt[:, a, 0:512], start=True, stop=True),
                                     r=[dk('kw'), ('Vt', a)], w=['ps%d' % (4 + c)])
                            for c in range(2):
                                S.pe(lambda e, kw=kw, c=c, a=a: e.matmul(P3[:, 8 + c:9 + c], lhsT=kw[:, c * 128:(c + 1) * 128],
                                                                         rhs=Vt[:, a, 512:513], start=True, stop=True),
                                     r=[dk('kw'), ('Vt', a)], w=['ps3'])
                            for c in range(2):
                                S.dve(lambda e, c=c, d=d, ebt=ebt, last=last: e.scalar_tensor_tensor(
                                    out=Cst[d][:, c, :], in0=Cst[d][:, c, :], scalar=ebt[:, last:last + 1],
                                    in1=PS[4 + c][:, :], op0=ALU.mult, op1=ALU.add),
                                    r=[('C', d), dk('ebt'), 'ps%d' % (4 + c)], w=[('C', d)])
                            S.dve(lambda e, d=d, ebt=ebt, last=last: e.scalar_tensor_tensor(
                                out=Nst[d][:], in0=Nst[d][:], scalar=ebt[:, last:last + 1], in1=P3[:, 8:10],
                                op0=ALU.mult, op1=ALU.add), r=[('N', d), dk('ebt'), 'ps3'], w=[('N', d)])
                            S.pool(lambda e, d=d: e.tensor_copy(out=Cb[d][:], in_=Cst[d][:]), r=[('C', d)], w=[('Cb', d)])
                            S.pool(lambda e, d=d: e.tensor_copy(out=Nb[d][:], in_=Nst[d][:]), r=[('N', d)], w=[('Nb', d)])
                    S.barrier()
                with ExitStack() as fes:
                    wo = self.sb(fes, "mlwo", [128, KC, 512], BF16)
                    wz = self.sb(fes, "mlwz", [128, KC, 512], BF16)
                    self.load_w(wo[:], 'mlwo', dr['ml_w_in'], 4096 + h * 512, 512)
                    self.load_w(wz[:], 'mlwz', dr['ml_w_in'], 6144 + h * 512, 512)
                    junk = self.sb(fes, "mljunk", [128, 512], BF16)
                    ss = self.sb(fes, "mlss", [128, 2 * NT], F32)
                    so = self.sb(fes, "mlso", [128, 512], F32)
                    sz = self.sb(fes, "mlsz", [128, 512], F32)
                    hn = self.sb(fes, "mlhn", [128, 512], F32)
                    ub = self.sb(fes, "mlub", [128, 512], BF16)
                    uTt = [self.sb(fes, "mluT%d" % i, [128, 4, 128], BF16) for i in range(2)]
                    tiles = list(range(NT)) if need_ctx else list(range(NT_CTX, NT))
                    for n_, t in enumerate(tiles):
                        S.act(lambda e, t=t: e.activation(out=junk[:], in_=Hs[:, t, :], func=AF.Square,
                                                          accum_out=ss[:, 2 * t:2 * t + 1]),
                              r=[('Hs', t)], w=['mljunk', ('mlss', t)])
                        S.act(lambda e, t=t: e.activation(out=ss[:, 2 * t + 1:2 * t + 2], in_=ss[:, 2 * t:2 * t + 1],
                                                          func=AF.Sqrt, scale=1.0 / 512, bias=self.eps_col[:, 0:1]),
                              r=[('mlss', t), 'eps'], w=[('mlss1', t)])
                        S.dve(lambda e, t=t: e.reciprocal(out=ss[:, 2 * t:2 * t + 1], in_=ss[:, 2 * t + 1:2 * t + 2]),
                              r=[('mlss1', t)], w=[('mlss', t)])
                        self.proj_tm(PS[6][:, :], 'ps6', wo, 'mlwo', t, 0, 512)
                        self.proj_tm(PS[7][:, :], 'ps7', wz, 'mlwz', t, 0, 512)
                        S.act(lambda e: e.activation(out=so[:], in_=PS[6][:, :], func=AF.Sigmoid), r=['ps6'], w=['mlso'])
                        S.act(lambda e: e.activation(out=sz[:], in_=PS[7][:, :], func=AF.Silu), r=['ps7'], w=['mlsz'])
                        S.dve(lambda e, t=t, h=h: e.scalar_tensor_tensor(
                            out=hn[:], in0=Hs[:, t, :], scalar=ss[:, 2 * t:2 * t + 1], in1=hgB[:, h * 512:(h + 1) * 512],
                            op0=ALU.mult, op1=ALU.mult), r=[('Hs', t), ('mlss', t), 'mlhg'], w=['mlhn'])
                        S.pool(lambda e: e.tensor_tensor(out=so[:], in0=so[:], in1=sz[:], op=ALU.mult),
                               r=['mlso', 'mlsz'], w=['mlso'])
                        S.dve(lambda e: e.tensor_tensor(out=ub[:], in0=hn[:], in1=so[:], op=ALU.mult),
                              r=['mlhn', 'mlso'], w=['mlub'])
                        pj = n_ % 2
                        Pb = PS[pj][:].bitcast(BF16)
                        for k in range(4):
                            S.pe(lambda e, Pb=Pb, k=k: e.transpose(out=Pb[:, k * 128:(k + 1) * 128],
                                                                   in_=ub[:, k * 128:(k + 1) * 128], identity=self.ident_b[:]),
                                 r=['mlub', 'ident_b'], w=['ps%d' % pj])
                        ut = uTt[n_ % 2]
                        utk = 'mluT%d' % (n_ % 2)
                        S.act(lambda e, Pb=Pb, ut=ut: e.copy(out=ut[:], in_=Pb[:, 0:512].rearrange("p (c n) -> p c n", n=128)),
                              r=['ps%d' % pj], w=[utk])
                        S.dma(dr['ut'][h * 512:(h + 1) * 512, t * 128:(t + 1) * 128].rearrange("(c p) n -> p c n", p=128),
                              ut[:], r=[utk], w=[('utd', h, t)])
                    S.barrier()
        for h_ in range(4):
            head_body(h_)
        with ExitStack() as tes:
            self.wout = self.sb(tes, "wout", [128, 16, D], BF16)
            self.load_w(self.wout[:], 'wout', dr['ml_w_out'], 0, D, kc=16)
            uin = [self.sb(tes, "mluin%d" % i, [128, 16, 128], BF16) for i in range(2)]
            self._z_begin(tes)
            tiles = list(range(NT)) if need_ctx else list(range(NT_CTX, NT))
            for n_, t in enumerate(tiles):
                u = uin[n_ % 2]
                uk = 'mluin%d' % (n_ % 2)
                S.dma(u[:], dr['ut'][:, t * 128:(t + 1) * 128].rearrange("(c p) n -> p c n", p=128),
                      r=[('utd', hh, t) for hh in range(4)], w=[uk])
                self._z_tile(t, u, [uk], 'wout', 16)
            S.barrier()

    def _conv(self, les, l):
        S = self.S
        dr = self.dr
        PS = self.PS
        need_ctx = l < 3
        hT_ = self.hT
        uT = self.sb(les, "uT", [128, KC, T_ALL], BF16)
        self.wout = self.sb(les, "wout", [128, KC, D], BF16)
        cw = self.sb(les, "cw", [128, KC, 3], F32)
        cb = self.sb(les, "cb", [128, KC, 1], F32)
        for k_ in range(3):
            S.dma(cw[:, :, k_], dr['sc_conv_w'][k_, :].rearrange("(c p) -> p c", p=128), w=['cw'],
                  allow_slow_non_contiguous=True)
        S.dma(cb[:, :, 0], dr['sc_conv_b'][0, :].rearrange("(c p) -> p c", p=128), w=['cb'],
              allow_slow_non_contiguous=True)
        blocks = [(0, T_CTX)] + [(T_CTX + i * 512, 512) for i in range(4)]
        with ExitStack() as tes:
            wc = [self.sb(tes, "wc%d" % i, [128, KC, 512], BF16) for i in range(2)]
            abuf = [self.sb(tes, "abuf%d" % i, [128, T_ALL + 4], F32) for i in range(2)]
            ybuf = [self.sb(tes, "ybuf%d" % i, [128, T_ALL], F32) for i in range(2)]
            xin_sb = [self.sb(tes, "xin%d" % i, [128, 512], F32) for i in range(2)]
            sz = [self.sb(tes, "sz%d" % i, [128, 512], F32) for i in range(2)]
            bsb = [self.sb(tes, "bsb%d" % i, [128, 512], F32) for i in range(2)]
            for i in range(2):
                S.pool(lambda e, i=i: e.memset(abuf[i][:], 0.0), w=['abuf%d' % i])

            def acol(tok):
                return tok + 1 if tok < T_CTX else tok + 3

            for m in range(KC):
                w = wc[m % 2]
                wk = 'wc%d' % (m % 2)
                for q in range(4):
                    self.load_w(w[:, :, q * 128:(q + 1) * 128], wk, dr['sc_w_in'], q * D + m * 128, 128)
                ab = abuf[m % 2]
                abk = 'abuf%d' % (m % 2)
                yb = ybuf[m % 2]
                ybk = 'ybuf%d' % (m % 2)
                for bi, (t0, n) in enumerate(blocks):
                    j = bi % 2
                    Px, Pc = PS[0 + 2 * j], PS[1 + 2 * j]
                    for q, P in ((0, Px), (2, Pc)):
                        for k in range(KC):
                            S.pe(lambda e, P=P, q=q, k=k, t0=t0, n=n, w=w: e.matmul(
                                P[:, 0:n], lhsT=w[:, k, q * 128:(q + 1) * 128], rhs=hT_[:, k, t0:t0 + n],
                                start=(k == 0), stop=(k == KC - 1)),
                                r=[wk] + [('hT', tt) for tt in range(t0 // 128, (t0 + n) // 128)],
                                w=['ps%d' % (q // 2 + 2 * j)])
                    S.act(lambda e, Px=Px, j=j, n=n: e.copy(out=xin_sb[j][:, 0:n], in_=Px[:, 0:n]),
                          r=['ps%d' % (2 * j)], w=['xin%d' % j])
                    c0 = acol(t0)
                    S.dve(lambda e, Pc=Pc, j=j, n=n, c0=c0, ab=ab: e.tensor_tensor(
                        out=ab[:, c0:c0 + n], in0=Pc[:, 0:n], in1=xin_sb[j][:, 0:n], op=ALU.mult),
                        r=['ps%d' % (1 + 2 * j), 'xin%d' % j], w=[abk])
                for (t0, n) in ((0, T_CTX), (T_CTX, T_LAT)):
                    c0 = acol(t0)
                    S.dve(lambda e, ab=ab, yb=yb, c0=c0, n=n, t0=t0, m=m: e.tensor_scalar(
                        out=yb[:, t0:t0 + n], in0=ab[:, c0:c0 + n], scalar1=cw[:, m, 1:2], scalar2=cb[:, m, 0:1],
                        op0=ALU.mult, op1=ALU.add), r=[abk, 'cw', 'cb'], w=[ybk])
                    S.dve(lambda e, ab=ab, yb=yb, c0=c0, n=n, t0=t0, m=m: e.scalar_tensor_tensor(
                        out=yb[:, t0:t0 + n], in0=ab[:, c0 - 1:c0 - 1 + n], scalar=cw[:, m, 0:1], in1=yb[:, t0:t0 + n],
                        op0=ALU.mult, op1=ALU.add), r=[abk, 'cw', ybk], w=[ybk])
                    S.dve(lambda e, ab=ab, yb=yb, c0=c0, n=n, t0=t0, m=m: e.scalar_tensor_tensor(
                        out=yb[:, t0:t0 + n], in0=ab[:, c0 + 1:c0 + 1 + n], scalar=cw[:, m, 2:3], in1=yb[:, t0:t0 + n],
                        op0=ALU.mult, op1=ALU.add), r=[abk, 'cw', ybk], w=[ybk])
                for bi, (t0, n) in enumerate(blocks):
                    j = bi % 2
                    Pb, Pz = PS[4 + 2 * j], PS[5 + 2 * j]
                    for q, P in ((1, Pb), (3, Pz)):
                        for k in range(KC):
                            S.pe(lambda e, P=P, q=q, k=k, t0=t0, n=n, w=w: e.matmul(
                                P[:, 0:n], lhsT=w[:, k, q * 128:(q + 1) * 128], rhs=hT_[:, k, t0:t0 + n],
                                start=(k == 0), stop=(k == KC - 1)),
                                r=[wk] + [('hT', tt) for tt in range(t0 // 128, (t0 + n) // 128)],
                                w=['ps%d' % (4 + q // 2 + 2 * j)])
                    S.act(lambda e, Pz=Pz, j=j, n=n: e.activation(out=sz[j][:, 0:n], in_=Pz[:, 0:n], func=AF.Silu),
                          r=['ps%d' % (5 + 2 * j)], w=['sz%d' % j])
                    S.dve(lambda e, Pb=Pb, j=j, n=n, t0=t0, yb=yb: e.tensor_tensor(
                        out=bsb[j][:, 0:n], in0=Pb[:, 0:n], in1=yb[:, t0:t0 + n], op=ALU.mult),
                        r=['ps%d' % (4 + 2 * j), ybk], w=['bsb%d' % j])
                    S.pool(lambda e, j=j, n=n, t0=t0, m=m: e.tensor_tensor(
                        out=uT[:, m, t0:t0 + n], in0=bsb[j][:, 0:n], in1=sz[j][:, 0:n], op=ALU.mult),
                        r=['bsb%d' % j, 'sz%d' % j], w=[('uT', m, bi)])
            self.load_w(self.wout[:], 'wout', dr['sc_w_out'], 0, D)
            S.barrier()
        tiles = list(range(NT)) if need_ctx else list(range(NT_CTX, NT))

        def uT_fn(t):
            bi = 0 if t < NT_CTX else 1 + (t - NT_CTX) // 4
            return uT[:, :, t * 128:(t + 1) * 128], [('uT', m, bi) for m in range(KC)]
        self._phase_z(les, 'wout', KC, uT_fn, tiles)


def _rope_table():
    rows = T_LAT // 64
    row = np.repeat(np.arange(rows), 64).astype(np.float32)
    col = np.tile(np.arange(64), rows).astype(np.float32)
    n_freq = 16
    freqs = np.power(np.float32(10000.0), -np.arange(n_freq, dtype=np.float32) / np.float32(n_freq)).astype(np.float32)
    ang = np.concatenate([row[:, None] * freqs, col[:, None] * freqs], axis=-1).astype(np.float32)
    return np.concatenate([np.cos(ang), np.sin(ang)], axis=-1).astype(np.float32)


def _const_inputs():
    idx = np.arange(128)
    sel = np.zeros((2, 256), np.float32)
    sel[0, 0:128] = 1.0
    sel[1, 128:256] = 1.0
    return {
        'ident': np.eye(128, dtype=np.float32),
        'cf': np.concatenate([(idx[:, None] <= idx[None, :]), (idx[:, None] > idx[None, :])], axis=1).astype(np.float32),
        'cb': np.concatenate([(idx[:, None] >= idx[None, :]), (idx[:, None] < idx[None, :])], axis=1).astype(np.float32),
        'sel': sel,
        'rope': _rope_table(),
    }


def make_in_maps(inputs, x_all, ctx_all):
    f = lambda a: np.ascontiguousarray(np.asarray(a, dtype=np.float32))
    shared = dict(_const_inputs())
    for name, shape in W_SPECS:
        shared[name] = f(inputs[name]).reshape(shape)
    maps = []
    for b in range(8):
        m = dict(shared)
        m['x'] = f(x_all[b])
        m['ctx'] = f(ctx_all[b])
        m['c2'] = np.ascontiguousarray(np.stack([f(inputs['c'])[b], f(inputs['c_ctx'])], axis=0))
        maps.append(m)
    return maps


_PROG_CACHE = {}


def run_layers(inputs, layers, x_all=None, ctx_all=None, trace=False):
    key = tuple(layers)
    if key not in _PROG_CACHE:
        p = Prog(list(layers))
        p.build()
        _PROG_CACHE[key] = p
    p = _PROG_CACHE[key]
    x_all = inputs['x'] if x_all is None else x_all
    ctx_all = inputs['ctx'] if ctx_all is None else ctx_all
    maps = make_in_maps(inputs, x_all, ctx_all)
    import os
    ncores = int(os.environ.get('KNCORES', '8'))
    res = run_bass_kernel_spmd(p.nc, maps[:ncores], core_ids=list(range(ncores)), **({'trace': True} if trace else {}))
    x_out = np.stack([r['out'] for r in res.results], axis=0)
    ctx_out = np.stack([r['ctxs'] for r in res.results], axis=0)
    return x_out, ctx_out, res


def kernel(**inputs):
    x_out, _, _ = run_layers(inputs, (0, 1, 2, 3))
    return x_out.astype(np.float32)
```

```python
import numpy as np
from contextlib import ExitStack
import concourse.bass as bass
import concourse.mybir as mybir
from concourse.bass_utils import run_bass_kernel_spmd

F32 = mybir.dt.float32
BF16 = mybir.dt.bfloat16
AF = mybir.ActivationFunctionType
ALU = mybir.AluOpType
AX = mybir.AxisListType

D = 1024
T_LAT = 2048
T_CTX = 256
T_ALL = T_LAT + T_CTX
NT = T_ALL // 128
NT_CTX = T_CTX // 128
EPS = 1e-6
KC = D // 128

COMPUTE = ('pe', 'act', 'dve', 'pool')
N_DMA_SEMS = 24


class _Op:
    __slots__ = ('eng', 'fn', 'deps', 'idx', 'is_dma', 'sig', 'prewait', 'need')


class Sched:
    def __init__(self, nc, es):
        self.nc = nc
        self.ops = []
        self.last_w = {}
        self.readers = {}
        self.last_on_eng = {}
        self.dma_since_bar = []
        self.engs = {'pe': nc.tensor, 'act': nc.scalar, 'dve': nc.vector,
                     'pool': nc.gpsimd, 'sp': nc.sync}
        self.esem = {e: es.enter_context(nc.semaphore('sem_' + e)) for e in COMPUTE}
        self.dsem = [es.enter_context(nc.semaphore('dsem%d' % i)) for i in range(N_DMA_SEMS)]

    def add(self, eng, fn, r=(), w=(), is_dma=False, extra=()):
        op = _Op()
        op.eng = eng
        op.fn = fn
        op.idx = len(self.ops)
        op.is_dma = is_dma
        op.sig = None
        op.prewait = None
        op.need = False
        deps = set(extra)
        for k in r:
            lw = self.last_w.get(k)
            if lw is not None:
                deps.add(lw)
        for k in w:
            lw = self.last_w.get(k)
            if lw is not None:
                deps.add(lw)
            rd = self.readers.get(k)
            if rd:
                for v in rd.values():
                    if isinstance(v, list):
                        deps.update(v)
                    else:
                        deps.add(v)
        for k in w:
            self.last_w[k] = op.idx
            self.readers[k] = {}
        for k in r:
            if k in w:
                continue
            rd = self.readers.setdefault(k, {})
            if is_dma:
                rd.setdefault('dma', []).append(op.idx)
            else:
                rd[eng] = op.idx
        deps.discard(op.idx)
        op.deps = deps
        self.ops.append(op)
        if is_dma:
            self.dma_since_bar.append(op.idx)
        else:
            self.last_on_eng[eng] = op.idx
        return op

    def pe(self, fn, r=(), w=()):
        return self.add('pe', fn, r, w)

    def act(self, fn, r=(), w=()):
        return self.add('act', fn, r, w)

    def dve(self, fn, r=(), w=()):
        return self.add('dve', fn, r, w)

    def pool(self, fn, r=(), w=()):
        return self.add('pool', fn, r, w)

    def dma(self, out, in_, r=(), w=(), **kw):
        return self.add('sp', lambda e: e.dma_start(out=out, in_=in_, **kw), r, w, is_dma=True)

    def barrier(self):
        ex = set(self.last_on_eng.values()) | set(self.dma_since_bar)
        for e in COMPUTE + ('sp',):
            self.add(e, None, extra=ex)
        self.dma_since_bar = []
        self.last_w = {}
        self.readers = {}

    def emit(self):
        ops = self.ops
        for op in ops:
            for d in op.deps:
                if op.eng == 'pe' and ops[d].eng == 'pe':
                    continue
                ops[d].need = True
        cnt = {e: 0 for e in COMPUTE}
        dval = [0] * N_DMA_SEMS
        ndma = 0
        for op in ops:
            if op.is_dma:
                j = ndma % N_DMA_SEMS
                ndma += 1
                if dval[j] > 0:
                    op.prewait = (('d', j), self.dsem[j], dval[j])
                dval[j] += 16
                op.sig = (('d', j), self.dsem[j], dval[j])
            elif op.need and op.fn is not None:
                cnt[op.eng] += 1
                op.sig = (op.eng, self.esem[op.eng], cnt[op.eng])
        know = {e: {} for e in self.engs}
        clocks = [None] * len(ops)
        nwait = 0
        for op in ops:
            e = self.engs[op.eng]
            kn = know[op.eng]
            waits = []
            for d in sorted(op.deps, reverse=True):
                dop = ops[d]
                if dop.sig is None:
                    continue
                key, sem, val = dop.sig
                if op.eng == 'pe' and dop.eng == 'pe' and not dop.is_dma:
                    continue
                if kn.get(key, 0) >= val:
                    continue
                waits.append((key, sem, val))
                ck = clocks[d]
                if ck:
                    for k2, v2 in ck.items():
                        if kn.get(k2, 0) < v2:
                            kn[k2] = v2
                kn[key] = max(kn.get(key, 0), val)
            if op.prewait is not None:
                key, sem, val = op.prewait
                if kn.get(key, 0) < val:
                    waits.append((key, sem, val))
                    kn[key] = val
            best = {}
            for key, sem, val in waits:
                if key not in best or best[key][1] < val:
                    best[key] = (sem, val)
            for key, (sem, val) in best.items():
                e.wait_ge(sem, val)
                nwait += 1
            if op.fn is not None:
                ins = op.fn(e)
                if op.sig is not None:
                    key, sem, val = op.sig
                    ins.then_inc(sem, 16 if op.is_dma else 1)
                    ck = dict(kn)
                    ck[key] = val
                    clocks[op.idx] = ck
                    if not op.is_dma:
                        pass
        sp = self.engs['sp']
        for j in range(N_DMA_SEMS):
            if dval[j] > 0 and know['sp'].get(('d', j), 0) < dval[j]:
                sp.wait_ge(self.dsem[j], dval[j])
        return len(ops), nwait, cnt, ndma


W_SPECS = [
    ('ada_w', [4, D, 3 * D]), ('ada_b', [4, 3 * D]), ('norm_g', [4, D]),
    ('ml_w_in', [D, 8208]), ('ml_gate_b', [1, 16]), ('ml_head_g', [1, 2048]), ('ml_w_out', [2048, D]),
    ('at_w_in', [D, 2560]), ('at_q_g', [1, 64]), ('at_k_g', [1, 64]), ('at_sink', [1, 16]), ('at_w_out', [D, D]),
    ('sc_w_in', [D, 4096]), ('sc_conv_w', [3, D]), ('sc_conv_b', [1, D]), ('sc_w_out', [D, D]),
    ('hg_w_in', [D, 5120]), ('hg_f_b', [2, D]), ('hg_lb', [2, 4, D]), ('hg_head_g', [1, D]), ('hg_w_out', [D, D]),
]


class Prog:
    def __init__(self, layers):
        self.layers = layers
        self.nc = nc = bass.Bass("TRN2", target_bir_lowering=False)
        self.es = ExitStack()
        self.dr = {}
        dr = self.dr
        dr['x'] = nc.dram_tensor("x", [T_LAT, D], F32, kind="ExternalInput").ap()
        dr['ctx'] = nc.dram_tensor("ctx", [T_CTX, D], F32, kind="ExternalInput").ap()
        dr['c2'] = nc.dram_tensor("c2", [2, D], F32, kind="ExternalInput").ap()
        for name, shape in W_SPECS:
            dr[name] = nc.dram_tensor(name, shape, F32, kind="ExternalInput").ap()
        dr['ident'] = nc.dram_tensor("ident", [128, 128], F32, kind="ExternalInput").ap()
        dr['cf'] = nc.dram_tensor("cf", [128, 256], F32, kind="ExternalInput").ap()
        dr['cb'] = nc.dram_tensor("cb", [128, 256], F32, kind="ExternalInput").ap()
        dr['osc0'] = nc.dram_tensor("osc0", [T_ALL, D], F32, kind="Internal").ap()
        dr['osc1'] = nc.dram_tensor("osc1", [T_ALL, D], F32, kind="Internal").ap()
        dr['sel'] = nc.dram_tensor("sel", [2, 256], F32, kind="ExternalInput").ap()
        dr['rope'] = nc.dram_tensor("rope", [T_LAT, 64], F32, kind="ExternalInput").ap()
        dr['out'] = nc.dram_tensor("out", [T_LAT, D], F32, kind="ExternalOutput").ap()
        dr['ctxs'] = nc.dram_tensor("ctxs", [T_CTX, D], F32, kind="ExternalOutput").ap()
        dr['ut'] = nc.dram_tensor("ut_scr", [2048, T_ALL], BF16, kind="Internal").ap()

    def sb(self, es, name, shape, dt):
        self._uid = getattr(self, '_uid', 0) + 1
        return es.enter_context(self.nc.sbuf_tensor("sb%d_%s" % (self._uid, name), shape, dt))

    def build(self):
        nc = self.nc
        es = self.es
        with es:
            self.S = S = Sched(nc, es)
            self.PS = [es.enter_context(nc.psum_tensor("ps%d" % i, [128, 512], F32)) for i in range(8)]
            self._consts(es)
            for li, l in enumerate(self.layers):
                with ExitStack() as les:
                    self._layer(les, l, first=(li == 0))
                    S.barrier()
                    stats = None
            stats = S.emit()
        self.stats = stats
        return nc

    def _consts(self, es):
        S = self.S
        dr = self.dr
        self.ident_f = self.sb(es, "ident_f", [128, 128], F32)
        self.ident_b = self.sb(es, "ident_b", [128, 128], BF16)
        self.cfb = [self.sb(es, "cf", [128, 256], F32), self.sb(es, "cb", [128, 256], F32)]
        self.ones_f = self.sb(es, "ones_f", [128, 128], F32)
        S.pool(lambda e: e.memset(self.ones_f[:], 1.0), w=['ones_f'])
        self.sel = self.sb(es, "sel", [2, 256], F32)
        self.scT = self.sb(es, "scT", [128, KC, 2], F32)
        S.dma(self.ident_f[:], dr['ident'], w=['ident_f'])
        S.dma(self.cfb[0][:], dr['cf'], w=['cfb'])
        S.dma(self.cfb[1][:], dr['cb'], w=['cfb'])
        S.dma(self.sel[:], dr['sel'], w=['sel'])
        S.dve(lambda e: e.tensor_copy(out=self.ident_b[:], in_=self.ident_f[:]), r=['ident_f'], w=['ident_b'])
        cT = self.sb(es, "cT", [128, KC, 2], F32)
        sg = self.sb(es, "c_sg", [128, KC, 2], F32)
        for r_ in range(2):
            S.dma(cT[:, :, r_], dr['c2'][r_, :].rearrange("(c p) -> p c", p=128), w=['cT'],
                  allow_slow_non_contiguous=True)
        S.act(lambda e: e.activation(out=sg[:], in_=cT[:], func=AF.Sigmoid), r=['cT'], w=['c_sg'])
        S.dve(lambda e: e.tensor_tensor(out=self.scT[:], in0=cT[:], in1=sg[:], op=ALU.mult), r=['cT', 'c_sg'], w=['scT'])

    def res_src(self, t, first):
        dr = self.dr
        if t < NT_CTX:
            base = dr['ctx'] if first else dr['ctxs']
            return base[t * 128:(t + 1) * 128, :]
        base = dr['x'] if first else dr['out']
        tt = t - NT_CTX
        return base[tt * 128:(tt + 1) * 128, :]

    def res_dst(self, t):
        dr = self.dr
        if t < NT_CTX:
            return dr['ctxs'][t * 128:(t + 1) * 128, :]
        tt = t - NT_CTX
        return dr['out'][tt * 128:(tt + 1) * 128, :]

    def load_w(self, dst, dst_key, wdram, c0, n, kc=KC, step=256):
        S = self.S
        for k0 in range(0, kc, 8):
            for j0 in range(0, n, step):
                m = min(step, n - j0)
                i = self._wst_i % len(self.wst)
                self._wst_i += 1
                st = self.wst[i]
                key = 'wst%d' % i
                S.dma(st[:, :, 0:m],
                      wdram[k0 * 128:(k0 + 8) * 128, c0 + j0:c0 + j0 + m].rearrange("(c p) n -> p c n", p=128),
                      w=[key])
                S.pool(lambda e, st=st, m=m, j0=j0, k0=k0: e.tensor_copy(out=dst[:, k0:k0 + 8, j0:j0 + m],
                                                                       in_=st[:, :, 0:m]),
                       r=[key], w=[dst_key])

    def _phase_a(self, les, l, first):
        S = self.S
        dr = self.dr
        PS = self.PS
        nc = self.nc
        self.wst = [self.sb(les, "wst%d" % i, [128, 8, 256], F32) for i in range(2)]
        self._wst_i = 0
        self.hT = hT_ = self.sb(les, "hT", [128, KC, T_ALL], BF16)
        self.gate = gate_ = [self.sb(les, "gate%d" % r, [128, D], F32) for r in range(2)]
        with ExitStack() as tes:
            mod = self.sb(tes, "mod", [2, 3 * D], F32)
            adab = self.sb(tes, "adab", [2, 3 * D], F32)
            self.modb = modb_ = [self.sb(tes, "modb%d" % r, [128, 3 * D], F32) for r in range(2)]
            ng = self.sb(tes, "ng", [128, D], F32)
            S.dma(adab[:], dr['ada_b'][l:l + 1, :].partition_broadcast(2), w=['adab'])
            S.dma(ng[:], dr['norm_g'][l:l + 1, :].partition_broadcast(128), w=['ng'])
            adw = [self.sb(tes, "adw%d" % i, [128, KC, 512], F32) for i in range(2)]
            for n in range(6):
                a = adw[n % 2]
                ak = 'adw%d' % (n % 2)
                S.dma(a[:], dr['ada_w'][l, :, n * 512:(n + 1) * 512].rearrange("(c p) n -> p c n", p=128), w=[ak])
                pk = 'ps0'
                for k in range(KC):
                    S.pe(lambda e, a=a, k=k: e.matmul(PS[0][0:2, :], lhsT=self.scT[:, k, :], rhs=a[:, k, :],
                                                      start=(k == 0), stop=(k == KC - 1)),
                         r=[ak, 'scT'], w=[pk])
                S.dve(lambda e, n=n: e.tensor_tensor(out=mod[:, n * 512:(n + 1) * 512], in0=PS[0][0:2, :],
                                                     in1=adab[:, n * 512:(n + 1) * 512], op=ALU.add),
                      r=[pk, 'adab'], w=['mod'])
            S.dve(lambda e: e.tensor_scalar_add(out=mod[:, D:2 * D], in0=mod[:, D:2 * D], scalar1=1.0),
                  r=['mod'], w=['mod'])
            for r_ in range(2):
                for n in range(6):
                    pk = 'ps%d' % (1 + n % 2)
                    P = PS[1 + n % 2]
                    S.pe(lambda e, P=P, r_=r_, n=n: e.matmul(P[:, :], lhsT=self.sel[:, r_ * 128:(r_ + 1) * 128],
                                                             rhs=mod[:, n * 512:(n + 1) * 512], start=True, stop=True),
                         r=['sel', 'mod'], w=[pk])
                    S.act(lambda e, P=P, r_=r_, n=n: e.copy(out=modb_[r_][:, n * 512:(n + 1) * 512], in_=P[:, :]),
                          r=[pk], w=['modb%d' % r_])
                S.dve(lambda e, r_=r_: e.tensor_tensor(out=modb_[r_][:, D:2 * D], in0=modb_[r_][:, D:2 * D],
                                                       in1=ng[:], op=ALU.mult),
                      r=['modb%d' % r_, 'ng'], w=['modb%d' % r_])
                S.pool(lambda e, r_=r_: e.tensor_copy(out=gate_[r_][:], in_=modb_[r_][:, 2 * D:3 * D]),
                       r=['modb%d' % r_], w=['gate%d' % r_])
            xts = [self.sb(tes, "xt%d" % i, [128, D], F32) for i in range(2)]
            junk = self.sb(tes, "junk", [128, D], BF16)
            h1 = [self.sb(tes, "h1_%d" % i, [128, D], F32) for i in range(2)]
            hb = [self.sb(tes, "hb_%d" % i, [128, D], BF16) for i in range(2)]
            ss = self.sb(tes, "ss", [128, 2 * NT], F32)
            for t in range(NT):
                i = t % 2
                xt = xts[i]
                xk = 'xt%d' % i
                mb = modb_[1 if t < NT_CTX else 0]
                mk = 'modb%d' % (1 if t < NT_CTX else 0)
                S.dma(xt[:], self.res_src(t, first), r=[('res', t)], w=[xk])
                S.act(lambda e, xt=xt, t=t: e.activation(out=junk[:], in_=xt[:], func=AF.Square,
                                                         accum_out=ss[:, 2 * t:2 * t + 1]),
                      r=[xk], w=['junk', ('ss', t)])
                S.act(lambda e, t=t: e.activation(out=ss[:, 2 * t + 1:2 * t + 2], in_=ss[:, 2 * t:2 * t + 1],
                                                  func=AF.Sqrt, scale=1.0 / D, bias=self.eps_col[:, 0:1]),
                      r=[('ss', t), 'eps'], w=[('ss1', t)])
                S.dve(lambda e, t=t: e.reciprocal(out=ss[:, 2 * t:2 * t + 1], in_=ss[:, 2 * t + 1:2 * t + 2]),
                      r=[('ss1', t)], w=[('ss', t)])
                S.dve(lambda e, xt=xt, t=t, i=i, mb=mb: e.scalar_tensor_tensor(
                    out=h1[i][:], in0=xt[:], scalar=ss[:, 2 * t:2 * t + 1], in1=mb[:, D:2 * D],
                    op0=ALU.mult, op1=ALU.mult), r=[xk, ('ss', t), mk], w=['h1_%d' % i])
                S.pool(lambda e, i=i, mb=mb: e.tensor_tensor(out=hb[i][:], in0=h1[i][:], in1=mb[:, 0:D], op=ALU.add),
                       r=['h1_%d' % i, mk], w=['hb_%d' % i])
                pj = 2 + (t % 2)
                Pb = PS[pj][:].bitcast(BF16)
                for k in range(KC):
                    S.pe(lambda e, Pb=Pb, i=i, k=k: e.transpose(out=Pb[:, k * 128:(k + 1) * 128],
                                                                in_=hb[i][:, k * 128:(k + 1) * 128],
                                                                identity=self.ident_b[:]),
                         r=['hb_%d' % i, 'ident_b'], w=['ps%d' % pj])
                S.act(lambda e, Pb=Pb, t=t: e.copy(out=hT_[:, :, t * 128:(t + 1) * 128],
                                                   in_=Pb.rearrange("p (c n) -> p c n", c=KC)),
                      r=['ps%d' % pj], w=[('hT', t)])
            S.barrier()

    def _z_begin(self, tes):
        self._zx = [self.sb(tes, "zx%d" % i, [128, D], F32) for i in range(2)]
        self._ztmp = [self.sb(tes, "ztmp%d" % i, [128, D], F32) for i in range(2)]
        self._zn = 0

    def _z_tile(self, t, uT, ukeys, wout_key, n_kc):
        S = self.S
        PS = self.PS
        xr, tmp = self._zx, self._ztmp
        i = self._zn % 2
        self._zn += 1
        gt = self.gate[1 if t < NT_CTX else 0]
        wout = self.wout
        mk = 'gate%d' % (1 if t < NT_CTX else 0)
        S.dma(xr[i][:], self.res_cur(t), r=[('res', t)], w=['zx%d' % i])
        for half in range(2):
            pj = 4 + 2 * i + half
            P = PS[pj]
            for k in range(n_kc):
                S.pe(lambda e, P=P, k=k, half=half: e.matmul(
                    P[:, :], lhsT=uT[:, k, :], rhs=wout[:, k, half * 512:(half + 1) * 512],
                    start=(k == 0), stop=(k == n_kc - 1)), r=list(ukeys) + [wout_key], w=['ps%d' % pj])
            S.dve(lambda e, P=P, half=half: e.tensor_tensor(
                out=tmp[i][:, half * 512:(half + 1) * 512], in0=P[:, :],
                in1=gt[:, half * 512:(half + 1) * 512], op=ALU.mult),
                r=['ps%d' % pj, mk], w=[('ztmp', i, half)])
        S.pool(lambda e: e.tensor_tensor(out=xr[i][:], in0=xr[i][:], in1=tmp[i][:], op=ALU.add),
               r=['zx%d' % i, ('ztmp', i, 0), ('ztmp', i, 1)], w=['zx%d' % i])
        S.dma(self.res_dst(t), xr[i][:], r=['zx%d' % i], w=[('res', t)])

    def _phase_z(self, les, wout_key, n_kc, uT_fn, tiles):
        with ExitStack() as tes:
            self._z_begin(tes)
            for t in tiles:
                uT, ukeys = uT_fn(t)
                self._z_tile(t, uT, ukeys, wout_key, n_kc)
            self.S.barrier()

    def _layer(self, les, l, first):
        self._first = first
        self.res_cur = lambda t: self.res_src(t, first)
        if not hasattr(self, 'eps_col'):
            self.eps_col = self.sb(self.es, "eps_col", [128, 1], F32)
            self.S.dve(lambda e: e.memset(self.eps_col[:], EPS), w=['eps'])
        self._phase_a(les, l, first)
        kind = l % 4
        if kind == 2:
            self._conv(les, l)
        elif kind == 3:
            self._hgrn(les, l)
        elif kind == 1:
            self._attn(les, l)
        elif kind == 0:
            self._mlstm(les, l)
        else:
            raise NotImplementedError


    def proj_tm(self, P, pkey, w, wkey, t, c0, n):
        hT = self.hT
        for k in range(KC):
            self.S.pe(lambda e, k=k: e.matmul(P, lhsT=hT[:, k, t * 128:(t + 1) * 128], rhs=w[:, k, c0:c0 + n],
                                              start=(k == 0), stop=(k == KC - 1)),
                      r=[wkey, ('hT', t)], w=[pkey])

    def proj_fm(self, P, pkey, w, wkey, t0, nt, c0, m=128):
        keys = [('hT', tt) for tt in range(t0 // 128, (t0 + nt + 127) // 128)]
        hT = self.hT
        for k in range(KC):
            self.S.pe(lambda e, k=k: e.matmul(P, lhsT=w[:, k, c0:c0 + m], rhs=hT[:, k, t0:t0 + nt],
                                              start=(k == 0), stop=(k == KC - 1)),
                      r=[wkey] + keys, w=[pkey])

    def _hgrn(self, les, l):
        S = self.S
        dr = self.dr
        PS = self.PS
        need_ctx = l < 3
        osc = [dr['osc0'], dr['osc1']]
        order = [list(range(NT)), [1, 0] + list(range(NT - 1, NT_CTX - 1, -1))]
        def group_body(g):
            with ExitStack() as ges:
                W = {}
                for nm, c0 in (('q', 0), ('f0', D), ('f1', 2 * D), ('i', 3 * D)):
                    W[nm] = self.sb(ges, "hgw_" + nm, [128, KC, 512], BF16)
                    self.load_w(W[nm][:], 'hgw_' + nm, dr['hg_w_in'], c0 + g * 512, 512)
                lbB = [self.sb(ges, "lbB%d" % d, [128, 512], F32) for d in range(2)]
                fbB = [self.sb(ges, "fbB%d" % d, [128, 512], F32) for d in range(2)]
                with ExitStack() as tes:
                    raw = self.sb(tes, "lbraw", [128, 4, 512], F32)
                    den = self.sb(tes, "lbden", [128, 512], F32)
                    for d in range(2):
                        for j in range(4):
                            S.dma(raw[:, j, :], dr['hg_lb'][d, j:j + 1, g * 512:(g + 1) * 512].partition_broadcast(128),
                                  w=['lbraw'])
                        S.dma(fbB[d][:], dr['hg_f_b'][d:d + 1, g * 512:(g + 1) * 512].partition_broadcast(128),
                              w=['fbB%d' % d])
                        S.act(lambda e: e.activation(out=raw[:], in_=raw[:], func=AF.Exp), r=['lbraw'], w=['lbraw'])
                        S.dve(lambda e: e.tensor_tensor(out=den[:], in0=raw[:, 0, :], in1=raw[:, 1, :], op=ALU.add),
                              r=['lbraw'], w=['lbden'])
                        S.dve(lambda e: e.tensor_tensor(out=den[:], in0=den[:], in1=raw[:, 2, :], op=ALU.add),
                              r=['lbraw', 'lbden'], w=['lbden'])
                        S.dve(lambda e: e.tensor_tensor(out=den[:], in0=den[:], in1=raw[:, 3, :], op=ALU.add),
                              r=['lbraw', 'lbden'], w=['lbden'])
                        S.dve(lambda e: e.reciprocal(out=den[:], in_=den[:]), r=['lbden'], w=['lbden'])
                        S.dve(lambda e, d=d: e.memset(lbB[d][:], 0.0), w=['lbB%d' % d])
                        for j in range(1, l + 1):
                            S.dve(lambda e, d=d, j=j: e.tensor_tensor(out=lbB[d][:], in0=lbB[d][:], in1=raw[:, j, :],
                                                                      op=ALU.add), r=['lbraw', 'lbB%d' % d],
                                  w=['lbB%d' % d])
                        S.dve(lambda e, d=d: e.tensor_tensor(out=lbB[d][:], in0=lbB[d][:], in1=den[:], op=ALU.mult),
                              r=['lbden', 'lbB%d' % d], w=['lbB%d' % d])
                    S.barrier()
                St = [[self.sb(ges, "hgS%d_%d" % (d, h), [128, 128], F32) for h in range(4)] for d in range(2)]
                Sb = [[self.sb(ges, "hgSb%d_%d" % (d, h), [128, 128], BF16) for h in range(4)] for d in range(2)]
                for d in range(2):
                    for h in range(4):
                        S.pool(lambda e, d=d, h=h: e.memset(St[d][h][:], 0.0), w=[('St', d, h)])
                        S.pool(lambda e, d=d, h=h: e.memset(Sb[d][h][:], 0.0), w=[('Sb', d, h)])
                def wt(nm, shape, dt):
                    return [self.sb(ges, "%s%d" % (nm, d), shape, dt) for d in range(2)]
                t0_ = wt("hg_t0", [128, 512], F32)
                sg_ = wt("hg_sg", [128, 512], F32)
                om_ = wt("hg_om", [128, 512], F32)
                lf_ = wt("hg_lf", [128, 512], F32)
                kk_ = wt("hg_kk", [128, 512], F32)
                er_ = wt("hg_er", [128, 512], F32)
                ke_ = wt("hg_ke", [128, 512], BF16)
                ii_ = wt("hg_ii", [128, 512], BF16)
                ot_ = wt("hg_ot", [128, 512], F32)
                qs_ = wt("hg_qs", [128, 512], F32)
                ea_ = wt("hg_ea", [128, 512], F32)
                ed_ = wt("hg_ed", [128, 512], F32)
                qe_ = wt("hg_qe", [128, 512], BF16)
                qd_ = wt("hg_qd", [128, 512], BF16)
                kdT_ = wt("hg_kdT", [128, 512], BF16)
                atm_ = wt("hg_atm", [128, 512], BF16)
                for step in range(NT):
                    for d in range(2):
                        a = order[d][step]
                        C = self.cfb[d]
                        last = 127 if d == 0 else 0
                        dk = lambda nm: ('hg', nm, d)
                        t0, sg, om, lf, kk, er, ke, ii, ot = (x[d] for x in (t0_, sg_, om_, lf_, kk_, er_, ke_, ii_, ot_))
                        qs, ea, ed, qe, qd, kdT, atm = (x[d] for x in (qs_, ea_, ed_, qe_, qd_, kdT_, atm_))
                        fw = W['f%d' % d]
                        self.proj_tm(PS[0][:, :], 'ps0', fw, 'hgw_f%d' % d, a, 0, 512)
                        S.dve(lambda e, t0=t0, d=d: e.tensor_tensor(out=t0[:], in0=PS[0][:, :], in1=fbB[d][:], op=ALU.add),
                              r=['ps0', 'fbB%d' % d], w=[dk('t0')])
                        S.act(lambda e, t0=t0, sg=sg: e.activation(out=sg[:], in_=t0[:], func=AF.Sigmoid),
                              r=[dk('t0')], w=[dk('sg')])
                        S.dve(lambda e, sg=sg, om=om: e.tensor_scalar(out=om[:], in0=sg[:], scalar1=-1.0, scalar2=1.0,
                                                                      op0=ALU.mult, op1=ALU.add),
                              r=[dk('sg')], w=[dk('om')])
                        S.pool(lambda e, om=om, d=d: e.tensor_tensor(out=om[:], in0=om[:], in1=lbB[d][:], op=ALU.mult),
                               r=[dk('om'), 'lbB%d' % d], w=[dk('om')])
                        S.dve(lambda e, om=om, sg=sg: e.tensor_tensor(out=sg[:], in0=om[:], in1=sg[:], op=ALU.add),
                              r=[dk('om'), dk('sg')], w=[dk('sg')])
                        S.act(lambda e, sg=sg, lf=lf: e.activation(out=lf[:], in_=sg[:], func=AF.Ln),
                              r=[dk('sg')], w=[dk('lf')])
                        S.pool(lambda e, sg=sg, kk=kk: e.tensor_scalar(out=kk[:], in0=sg[:], scalar1=-1.0, scalar2=1.0,
                                                                       op0=ALU.mult, op1=ALU.add),
                               r=[dk('sg')], w=[dk('kk')])
                        S.pe(lambda e, C=C, lf=lf: e.matmul(PS[1][:, :], lhsT=C[:, 128:256], rhs=lf[:], start=True, stop=True),
                             r=['cfb', dk('lf')], w=['ps1'])
                        S.act(lambda e, er=er: e.activation(out=er[:], in_=PS[1][:, :], func=AF.Exp),
                              r=['ps1'], w=[dk('er')])
                        S.dve(lambda e, kk=kk, er=er, ke=ke: e.tensor_tensor(out=ke[:], in0=kk[:], in1=er[:], op=ALU.mult),
                              r=[dk('kk'), dk('er')], w=[dk('ke')])
                        self.proj_tm(PS[0][:, :], 'ps0', W['i'], 'hgw_i', a, 0, 512)
                        S.act(lambda e, ii=ii: e.copy(out=ii[:], in_=PS[0][:, :]), r=['ps0'], w=[dk('ii')])
                        for hh in range(4):
                            self.proj_fm(PS[2][:, hh * 128:(hh + 1) * 128], 'ps2', W['q'], 'hgw_q', a * 128, 128, hh * 128)
                        S.act(lambda e, qs=qs: e.activation(out=qs[:], in_=PS[2][:, :], func=AF.Silu),
                              r=['ps2'], w=[dk('qs')])
                        for hh in range(4):
                            Pr = PS[3 + hh // 2]
                            S.pe(lambda e, Pr=Pr, hh=hh, lf=lf, C=C: e.matmul(
                                Pr[:, (hh % 2) * 256:(hh % 2) * 256 + 256], lhsT=lf[:, hh * 128:(hh + 1) * 128],
                                rhs=C[:, :], start=True, stop=True), r=[dk('lf'), 'cfb'], w=['ps%d' % (3 + hh // 2)])
                        for half in range(2):
                            Pr = PS[3 + half].rearrange("p (h two n) -> p h two n", two=2, n=128)
                            S.act(lambda e, Pr=Pr, ea=ea, half=half: e.activation(
                                out=ea[:, half * 256:(half + 1) * 256].rearrange("p (h n) -> p h n", n=128),
                                in_=Pr[:, :, 0, :], func=AF.Exp), r=['ps%d' % (3 + half)], w=[dk('ea')])
                            S.act(lambda e, Pr=Pr, ed=ed, half=half: e.activation(
                                out=ed[:, half * 256:(half + 1) * 256].rearrange("p (h n) -> p h n", n=128),
                                in_=Pr[:, :, 1, :], func=AF.Exp, scale=-1.0), r=['ps%d' % (3 + half)], w=[dk('ed')])
                        S.dve(lambda e, qs=qs, ea=ea, qe=qe: e.tensor_tensor(out=qe[:], in0=qs[:], in1=ea[:], op=ALU.mult),
                              r=[dk('qs'), dk('ea')], w=[dk('qe')])
                        S.pool(lambda e, qs=qs, ed=ed, qd=qd: e.tensor_tensor(out=qd[:], in0=qs[:], in1=ed[:], op=ALU.mult),
                               r=[dk('qs'), dk('ed')], w=[dk('qd')])
                        P5b = PS[5][:].bitcast(BF16)
                        for hh in range(4):
                            S.pe(lambda e, hh=hh, ke=ke: e.transpose(out=P5b[:, hh * 128:(hh + 1) * 128],
                                                                     in_=ke[:, hh * 128:(hh + 1) * 128],
                                                                     identity=self.ident_b[:]),
                                 r=[dk('ke'), 'ident_b'], w=['ps5'])
                        S.act(lambda e, kdT=kdT: e.copy(out=kdT[:], in_=P5b[:, 0:512]), r=['ps5'], w=[dk('kdT')])
                        for hh in range(4):
                            S.pe(lambda e, hh=hh, kdT=kdT, qd=qd: e.matmul(
                                PS[6][:, hh * 128:(hh + 1) * 128], lhsT=kdT[:, hh * 128:(hh + 1) * 128],
                                rhs=qd[:, hh * 128:(hh + 1) * 128], start=True, stop=True),
                                r=[dk('kdT'), dk('qd')], w=['ps6'])
                        S.dve(lambda e, atm=atm, C=C: e.tensor_tensor(
                            out=atm[:].rearrange("p (h n) -> p h n", n=128),
                            in0=PS[6][:].rearrange("p (h n) -> p h n", n=128),
                            in1=C[:, 0:128].unsqueeze(1).to_broadcast([128, 4, 128]), op=ALU.mult),
                            r=['ps6', 'cfb'], w=[dk('atm')])
                        for hh in range(4):
                            S.pe(lambda e, hh=hh, atm=atm, ii=ii: e.matmul(
                                PS[7][:, hh * 128:(hh + 1) * 128], lhsT=atm[:, hh * 128:(hh + 1) * 128],
                                rhs=ii[:, hh * 128:(hh + 1) * 128], start=True, stop=False),
                                r=[dk('atm'), dk('ii')], w=['ps7'])
                            S.pe(lambda e, hh=hh, qe=qe, d=d: e.matmul(
                                PS[7][:, hh * 128:(hh + 1) * 128], lhsT=qe[:, hh * 128:(hh + 1) * 128],
                                rhs=Sb[d][hh][:], start=False, stop=True),
                                r=[dk('qe'), ('Sb', d, hh)], w=['ps7'])
                        S.act(lambda e, ot=ot: e.copy(out=ot[:], in_=PS[7][:, :]), r=['ps7'], w=[dk('ot')])
                        S.dma(osc[d][a * 128:(a + 1) * 128, g * 512:(g + 1) * 512], ot[:], r=[dk('ot')],
                              w=[('osc', d, a, g)])
                        for hh in range(4):
                            S.pe(lambda e, hh=hh, ke=ke, ii=ii: e.matmul(
                                PS[1][:, hh * 128:(hh + 1) * 128], lhsT=ke[:, hh * 128:(hh + 1) * 128],
                                rhs=ii[:, hh * 128:(hh + 1) * 128], start=True, stop=True),
                                r=[dk('ke'), dk('ii')], w=['ps1'])
                        for hh in range(4):
                            S.dve(lambda e, hh=hh, d=d, ea=ea, last=last: e.scalar_tensor_tensor(
                                out=St[d][hh][:], in0=St[d][hh][:], scalar=ea[:, hh * 128 + last:hh * 128 + last + 1],
                                in1=PS[1][:, hh * 128:(hh + 1) * 128], op0=ALU.mult, op1=ALU.add),
                                r=[('St', d, hh), dk('ea'), 'ps1'], w=[('St', d, hh)])
                            S.pool(lambda e, hh=hh, d=d: e.tensor_copy(out=Sb[d][hh][:], in_=St[d][hh][:]),
                                   r=[('St', d, hh)], w=[('Sb', d, hh)])
                S.barrier()
        for g_ in range(2):
            group_body(g_)
        tiles = list(range(NT)) if need_ctx else list(range(NT_CTX, NT))
        uT = self.sb(les, "uT", [128, KC, T_ALL], BF16)
        self.wout = self.sb(les, "wout", [128, KC, D], BF16)
        with ExitStack() as tes:
            wz = self.sb(tes, "hgw_z", [128, KC, D], BF16)
            self.load_w(wz[:], 'hgw_z', dr['hg_w_in'], 4 * D, D)
            self.load_w(self.wout[:], 'wout', dr['hg_w_out'], 0, D)
            hg = self.sb(tes, "hg_g", [128, D], F32)
            S.dma(hg[:], dr['hg_head_g'][0:1, :].partition_broadcast(128), w=['hg_g'])
            o0 = [self.sb(tes, "fo0_%d" % i, [128, D], F32) for i in range(2)]
            o1 = [self.sb(tes, "fo1_%d" % i, [128, D], F32) for i in range(2)]
            sq = self.sb(tes, "fsq", [128, D], F32)
            ssq = self.sb(tes, "fssq", [128, 2 * 8 * NT], F32)
            szt = self.sb(tes, "fsz", [128, D], F32)
            ub = [self.sb(tes, "fub%d" % i, [128, D], BF16) for i in range(2)]
            for n_, t in enumerate(tiles):
                i = n_ % 2
                S.dma(o0[i][:], osc[0][t * 128:(t + 1) * 128, :], r=[('osc', 0, t, 0), ('osc', 0, t, 1)], w=['fo0_%d' % i])
                S.dma(o1[i][:], osc[1][t * 128:(t + 1) * 128, :], r=[('osc', 1, t, 0), ('osc', 1, t, 1)], w=['fo1_%d' % i])
                S.dve(lambda e, i=i: e.tensor_tensor(out=o0[i][:], in0=o0[i][:], in1=o1[i][:], op=ALU.add),
                      r=['fo0_%d' % i, 'fo1_%d' % i], w=['fo0_%d' % i])
                S.pool(lambda e, i=i: e.tensor_tensor(out=sq[:], in0=o0[i][:], in1=o0[i][:], op=ALU.mult),
                       r=['fo0_%d' % i], w=['fsq'])
                c0 = 16 * n_
                S.dve(lambda e, c0=c0: e.tensor_reduce(out=ssq[:, c0:c0 + 8], in_=sq[:].rearrange("p (h n) -> p h n", n=128),
                                                       axis=AX.X, op=ALU.add), r=['fsq'], w=[('fssq', n_)])
                S.act(lambda e, c0=c0: e.activation(out=ssq[:, c0 + 8:c0 + 16], in_=ssq[:, c0:c0 + 8], func=AF.Sqrt,
                                                    scale=1.0 / 128, bias=self.eps_col[:, 0:1]),
                      r=[('fssq', n_), 'eps'], w=[('fssq1', n_)])
                S.dve(lambda e, c0=c0: e.reciprocal(out=ssq[:, c0:c0 + 8], in_=ssq[:, c0 + 8:c0 + 16]),
                      r=[('fssq1', n_)], w=[('fssq', n_)])
                S.dve(lambda e, i=i, c0=c0: e.tensor_tensor(
                    out=o0[i][:].rearrange("p (h n) -> p h n", n=128), in0=o0[i][:].rearrange("p (h n) -> p h n", n=128),
                    in1=ssq[:, c0:c0 + 8].unsqueeze(2).to_broadcast([128, 8, 128]), op=ALU.mult),
                    r=['fo0_%d' % i, ('fssq', n_)], w=['fo0_%d' % i])
                S.pool(lambda e, i=i: e.tensor_tensor(out=o0[i][:], in0=o0[i][:], in1=hg[:], op=ALU.mult),
                       r=['fo0_%d' % i, 'hg_g'], w=['fo0_%d' % i])
                for half in range(2):
                    self.proj_tm(PS[half][:, :], 'ps%d' % half, wz, 'hgw_z', t, half * 512, 512)
                    S.act(lambda e, half=half: e.activation(out=szt[:, half * 512:(half + 1) * 512], in_=PS[half][:, :],
                                                            func=AF.Silu), r=['ps%d' % half], w=[('fsz', half)])
                S.dve(lambda e, i=i: e.tensor_tensor(out=ub[i][:], in0=o0[i][:], in1=szt[:], op=ALU.mult),
                      r=['fo0_%d' % i, ('fsz', 0), ('fsz', 1)], w=['fub%d' % i])
                pj = 2 + i
                Pb = PS[pj][:].bitcast(BF16)
                for k in range(KC):
                    S.pe(lambda e, Pb=Pb, i=i, k=k: e.transpose(out=Pb[:, k * 128:(k + 1) * 128],
                                                                in_=ub[i][:, k * 128:(k + 1) * 128],
                                                                identity=self.ident_b[:]),
                         r=['fub%d' % i, 'ident_b'], w=['ps%d' % pj])
                S.act(lambda e, Pb=Pb, t=t: e.copy(out=uT[:, :, t * 128:(t + 1) * 128],
                                                   in_=Pb.rearrange("p (c n) -> p c n", c=KC)),
                      r=['ps%d' % pj], w=[('uT', t)])
            S.barrier()

        def uT_fn(t):
            return uT[:, :, t * 128:(t + 1) * 128], [('uT', t)]
        self._phase_z(les, 'wout', KC, uT_fn, tiles)


    def _attn(self, les, l):
        S = self.S
        dr = self.dr
        PS = self.PS
        need_ctx = l < 3
        QT = self.sb(les, "QT", [128, 8, T_ALL], BF16)
        KT = self.sb(les, "KT", [128, 4, T_ALL], BF16)
        Va = self.sb(les, "Vaug", [128, NT, 4 * 72], BF16)
        esink = self.sb(les, "esink", [128, 16], F32)
        maskP = self.sb(les, "maskP", [128, 128], BF16)
        maskN = self.sb(les, "maskN", [128, 128], BF16)
        S.dma(esink[:], dr['at_sink'][0:1, :].partition_broadcast(128), w=['esink'])
        S.act(lambda e: e.activation(out=esink[:], in_=esink[:], func=AF.Exp), r=['esink'], w=['esink'])
        S.dve(lambda e: e.tensor_scalar_mul(out=maskP[:], in0=self.cfb[1][:, 128:256], scalar1=-30000.0),
              r=['cfb'], w=['maskP'])
        S.dve(lambda e: e.tensor_scalar_mul(out=maskN[:], in0=self.cfb[0][:, 128:256], scalar1=-30000.0),
              r=['cfb'], w=['maskN'])
        S.pool(lambda e: e.memset(Va[:], 1.0), w=['Va_init'])
        with ExitStack() as tes:
            wqk = self.sb(tes, "wqk", [128, KC, 1536], BF16)
            self.load_w(wqk[:], 'wqk', dr['at_w_in'], 0, 1536)
            G = self.sb(tes, "qkG", [128, 1280], F32)
            for h in range(16):
                S.dma(G[:, h * 64:(h + 1) * 64], dr['at_q_g'][0:1, :].partition_broadcast(128), w=['qkG'])
            for j in range(4):
                S.dma(G[:, 1024 + j * 64:1024 + (j + 1) * 64], dr['at_k_g'][0:1, :].partition_broadcast(128), w=['qkG'])
            S.dve(lambda e: e.tensor_scalar_mul(out=G[:, 0:1024], in0=G[:, 0:1024], scalar1=0.125), r=['qkG'], w=['qkG'])
            qk = self.sb(tes, "qk", [128, 1280], F32)
            sq = self.sb(tes, "qksq", [128, 1280], F32)
            ssq = self.sb(tes, "qkss", [128, 40 * NT], F32)
            qkr = self.sb(tes, "qkr", [128, 1280], BF16)
            kdup = self.sb(tes, "kdup", [128, 512], BF16)
            tt = [self.sb(tes, "rt%d" % i, [128, 640], F32) for i in range(4)]
            cs = [self.sb(tes, "cs%d" % i, [128, 64], F32) for i in range(2)]
            for t in range(NT):
                self.proj_tm(PS[0][:, :], 'ps0', wqk, 'wqk', t, 0, 512)
                self.proj_tm(PS[1][:, :], 'ps1', wqk, 'wqk', t, 512, 512)
                self.proj_tm(PS[2][:, :], 'ps2', wqk, 'wqk', t, 1024, 512)
                S.act(lambda e: e.copy(out=qk[:, 0:512], in_=PS[0][:, :]), r=['ps0'], w=[('qk', 0)])
                S.act(lambda e: e.copy(out=qk[:, 512:1024], in_=PS[1][:, :]), r=['ps1'], w=[('qk', 1)])
                S.act(lambda e: e.copy(out=qk[:, 1024:1280], in_=PS[2][:, 0:256]), r=['ps2'], w=[('qk', 2)])
                S.act(lambda e, t=t: e.copy(out=Va[:, t, :].rearrange("p (j c) -> p j c", c=72)[:, :, 0:64],
                                            in_=PS[2][:, 256:512].rearrange("p (j c) -> p j c", c=64)),
                      r=['ps2', 'Va_init'], w=[('Va', t)])
                qka = [('qk', 0), ('qk', 1), ('qk', 2)]
                import os
                dbg = os.environ.get('KDBG', '')
                if dbg == 'prepA':
                    continue
                S.pool(lambda e: e.tensor_tensor(out=sq[:], in0=qk[:], in1=qk[:], op=ALU.mult), r=qka, w=['qksq'])
                c0 = 40 * t
                S.dve(lambda e, c0=c0: e.tensor_reduce(out=ssq[:, c0:c0 + 20], in_=sq[:].rearrange("p (h n) -> p h n", n=64),
                                                       axis=AX.X, op=ALU.add), r=['qksq'], w=[('qkss', t)])
                S.act(lambda e, c0=c0: e.activation(out=ssq[:, c0 + 20:c0 + 40], in_=ssq[:, c0:c0 + 20], func=AF.Sqrt,
                                                    scale=1.0 / 64, bias=self.eps_col[:, 0:1]),
                      r=[('qkss', t), 'eps'], w=[('qkss1', t)])
                S.dve(lambda e, c0=c0: e.reciprocal(out=ssq[:, c0:c0 + 20], in_=ssq[:, c0 + 20:c0 + 40]),
                      r=[('qkss1', t)], w=[('qkss', t)])
                S.dve(lambda e, c0=c0: e.tensor_tensor(
                    out=qk[:].rearrange("p (h n) -> p h n", n=64), in0=qk[:].rearrange("p (h n) -> p h n", n=64),
                    in1=ssq[:, c0:c0 + 20].unsqueeze(2).to_broadcast([128, 20, 64]), op=ALU.mult),
                    r=qka + [('qkss', t)], w=qka)
                if dbg == 'prepB':
                    continue
                if t < NT_CTX:
                    S.pool(lambda e: e.tensor_tensor(out=qkr[:], in0=qk[:], in1=G[:], op=ALU.mult),
                           r=qka + ['qkG'], w=['qkr'])
                else:
                    S.pool(lambda e: e.tensor_tensor(out=qk[:], in0=qk[:], in1=G[:], op=ALU.mult),
                           r=qka + ['qkG'], w=qka)
                    c = cs[t % 2]
                    ck = 'cs%d' % (t % 2)
                    tl = t - NT_CTX
                    S.dma(c[:], dr['rope'][tl * 128:(tl + 1) * 128, :], w=[ck])
                    v4 = qk[:].rearrange("p (h two n) -> p h two n", two=2, n=32)
                    x1, x2 = v4[:, :, 0, :], v4[:, :, 1, :]
                    r4 = qkr[:].rearrange("p (h two n) -> p h two n", two=2, n=32)
                    cosb = c[:, 0:32].unsqueeze(1).to_broadcast([128, 20, 32])
                    sinb = c[:, 32:64].unsqueeze(1).to_broadcast([128, 20, 32])
                    tv = [x[:].rearrange("p (h n) -> p h n", n=32) for x in tt]
                    S.dve(lambda e, x1=x1, cosb=cosb, tv=tv: e.tensor_tensor(out=tv[0], in0=x1, in1=cosb, op=ALU.mult),
                          r=qka + [ck], w=['rt0'])
                    S.dve(lambda e, x2=x2, sinb=sinb, tv=tv: e.tensor_tensor(out=tv[1], in0=x2, in1=sinb, op=ALU.mult),
                          r=qka + [ck], w=['rt1'])
                    S.dve(lambda e, x1=x1, sinb=sinb, tv=tv: e.tensor_tensor(out=tv[2], in0=x1, in1=sinb, op=ALU.mult),
                          r=qka + [ck], w=['rt2'])
                    S.dve(lambda e, x2=x2, cosb=cosb, tv=tv: e.tensor_tensor(out=tv[3], in0=x2, in1=cosb, op=ALU.mult),
                          r=qka + [ck], w=['rt3'])
                    S.dve(lambda e, r4=r4, tv=tv: e.tensor_tensor(out=r4[:, :, 0, :], in0=tv[0], in1=tv[1], op=ALU.subtract),
                          r=['rt0', 'rt1'], w=[('qkr', 0)])
                    S.pool(lambda e, r4=r4, tv=tv: e.tensor_tensor(out=r4[:, :, 1, :], in0=tv[2], in1=tv[3], op=ALU.add),
                           r=['rt2', 'rt3'], w=[('qkr', 1)])
                qrk = ['qkr', ('qkr', 0), ('qkr', 1)]
                if dbg == 'prepC':
                    continue
                for dd in range(2):
                    S.pool(lambda e, dd=dd: e.tensor_copy(
                        out=kdup[:].rearrange("p (j two n) -> p j two n", two=2, n=64)[:, :, dd, :],
                        in_=qkr[:, 1024:1280].rearrange("p (j n) -> p j n", n=64)),
                        r=qrk, w=['kdup'])
                P3b = PS[3][:].bitcast(BF16)
                P4b = PS[4][:].bitcast(BF16)
                for b in range(8):
                    S.pe(lambda e, b=b: e.transpose(out=P3b[:, b * 128:(b + 1) * 128], in_=qkr[:, b * 128:(b + 1) * 128],
                                                    identity=self.ident_b[:]), r=qrk + ['ident_b'], w=['ps3'])
                for j in range(4):
                    S.pe(lambda e, j=j: e.transpose(out=P4b[:, j * 128:(j + 1) * 128], in_=kdup[:, j * 128:(j + 1) * 128],
                                                    identity=self.ident_b[:]), r=['kdup', 'ident_b'], w=['ps4'])
                S.act(lambda e, t=t: e.copy(out=QT[:, :, t * 128:(t + 1) * 128],
                                            in_=P3b.rearrange("p (c n) -> p c n", n=128)), r=['ps3'], w=[('QT', t)])
                S.act(lambda e, t=t: e.copy(out=KT[:, :, t * 128:(t + 1) * 128],
                                            in_=P4b[:, 0:512].rearrange("p (c n) -> p c n", n=128)), r=['ps4'], w=[('KT', t)])
            S.barrier()
        import os
        if os.environ.get('KDBG', '').startswith('prep'):
            return
        with ExitStack() as tes:
            wz = self.sb(tes, "at_wz", [128, KC, D], BF16)
            self.wout = self.sb(tes, "wout", [128, KC, D], BF16)
            self.load_w(wz[:], 'at_wz', dr['at_w_in'], 1536, D)
            self.load_w(self.wout[:], 'wout', dr['at_w_out'], 0, D)
            PT = [self.sb(tes, "PT%d" % i, [128, 512], BF16) for i in range(10)]
            oat = self.sb(tes, "oat", [128, D], F32)
            den = self.sb(tes, "aden", [128, 16], F32)
            szt = self.sb(tes, "asz", [128, D], F32)
            ub = self.sb(tes, "aub", [128, D], BF16)
            uTt = [self.sb(tes, "auT%d" % i, [128, KC, 128], BF16) for i in range(2)]
            self._z_begin(tes)
            tiles = list(range(NT)) if need_ctx else list(range(NT_CTX, NT))
            nst = 0
            dbg = os.environ.get('KDBG', '')
            for n_, t in enumerate(tiles):
                if t < NT_CTX:
                    keys = [(0, None), (1, None)]
                else:
                    keys = []
                    if t - 1 >= NT_CTX:
                        keys.append((t - 1, 'P'))
                    keys.append((t, None))
                    if t + 1 < NT:
                        keys.append((t + 1, 'N'))
                    keys += [(0, None), (1, None)]
                for j in range(4):
                    Po = PS[2 + j]
                    pok = 'ps%d' % (2 + j)
                    pts = []
                    for ki, (s_, mk) in enumerate(keys):
                        b = nst % 2
                        pi = nst % 10
                        nst += 1
                        pts.append(pi)
                        PsAB = (PS[b], PS[6 + b])
                        pskAB = ('ps%d' % b, 'ps%d' % (6 + b))
                        if os.environ.get('KNOMASK'):
                            mk = None
                        if mk is not None:
                            mt = maskP if mk == 'P' else maskN
                            for ab in range(2):
                                S.pe(lambda e, ab=ab, mt=mt, PsAB=PsAB: e.matmul(
                                    PsAB[ab][:, 0:256], lhsT=self.ident_b[:],
                                    rhs=mt[:].unsqueeze(1).to_broadcast([128, 2, 128]),
                                    start=True, stop=False), r=['ident_b', 'maskP', 'maskN'], w=[pskAB[ab]])
                        for gq in range(4):
                            h = 4 * j + gq
                            pb = (h % 2) * 64
                            ab = h % 2
                            S.pe(lambda e, PsAB=PsAB, ab=ab, gq=gq, h=h, pb=pb, s_=s_, mk=mk, j=j, t=t: e.matmul(
                                PsAB[ab][:, (gq // 2) * 128:(gq // 2 + 1) * 128],
                                lhsT=KT[pb:pb + 64, j, s_ * 128:(s_ + 1) * 128],
                                rhs=QT[pb:pb + 64, h // 2, t * 128:(t + 1) * 128],
                                start=(mk is None), stop=(mk is None or gq >= 2)),
                                r=[('KT', s_), ('QT', t)], w=[pskAB[ab]])
                        for ab in range(2):
                            S.act(lambda e, ab=ab, PsAB=PsAB, pi=pi: e.activation(
                                out=PT[pi][:, ab * 256:(ab + 1) * 256], in_=PsAB[ab][:, 0:256], func=AF.Exp),
                                r=[pskAB[ab]], w=[('PT', pi, ab)])
                    if dbg == 'attS':
                        continue
                    for gq in range(4):
                        for ki, (s_, mk) in enumerate(keys):
                            pi = pts[ki]
                            S.pe(lambda e, Po=Po, gq=gq, pi=pi, s_=s_, j=j, ki=ki, nk=len(keys): e.matmul(
                                Po[:, gq * 65:(gq + 1) * 65],
                                lhsT=PT[pi][:, (gq % 2) * 256 + (gq // 2) * 128:(gq % 2) * 256 + (gq // 2) * 128 + 128],
                                rhs=Va[:, s_, j * 72:j * 72 + 65], start=(ki == 0), stop=(ki == nk - 1)),
                                r=[('PT', pi, gq % 2), ('Va', s_)], w=[pok])
                    Pv = Po[:, 0:260].rearrange("p (g c) -> p g c", c=65)
                    S.dve(lambda e, Pv=Pv, j=j: e.tensor_tensor(out=den[:, 4 * j:4 * j + 4], in0=Pv[:, :, 64],
                                                                in1=esink[:, 4 * j:4 * j + 4], op=ALU.add),
                          r=[pok, 'esink'], w=[('aden', j)])
                    S.dve(lambda e, j=j: e.reciprocal(out=den[:, 4 * j:4 * j + 4], in_=den[:, 4 * j:4 * j + 4]),
                          r=[('aden', j)], w=[('aden', j)])
                    S.dve(lambda e, Pv=Pv, j=j: e.tensor_tensor(
                        out=oat[:, j * 256:(j + 1) * 256].rearrange("p (g c) -> p g c", c=64), in0=Pv[:, :, 0:64],
                        in1=den[:, 4 * j:4 * j + 4].unsqueeze(2).to_broadcast([128, 4, 64]), op=ALU.mult),
                        r=[pok, ('aden', j)], w=[('oat', j)])
                if dbg in ('attS', 'attPV'):
                    continue
                for half in range(2):
                    self.proj_tm(PS[6 + half][:, :], 'ps%d' % (6 + half), wz, 'at_wz', t, half * 512, 512)
                    S.act(lambda e, half=half: e.activation(out=szt[:, half * 512:(half + 1) * 512], in_=PS[6 + half][:, :],
                                                            func=AF.Silu), r=['ps%d' % (6 + half)], w=[('asz', half)])
                S.pool(lambda e: e.tensor_tensor(out=ub[:], in0=oat[:], in1=szt[:], op=ALU.mult),
                       r=[('oat', j) for j in range(4)] + [('asz', 0), ('asz', 1)], w=['aub'])
                P0b = PS[n_ % 2][:].bitcast(BF16)
                pk0 = 'ps%d' % (n_ % 2)
                for k in range(KC):
                    S.pe(lambda e, P0b=P0b, k=k: e.transpose(out=P0b[:, k * 128:(k + 1) * 128],
                                                             in_=ub[:, k * 128:(k + 1) * 128], identity=self.ident_b[:]),
                         r=['aub', 'ident_b'], w=[pk0])
                ut = uTt[n_ % 2]
                utk = 'auT%d' % (n_ % 2)
                S.act(lambda e, P0b=P0b, ut=ut: e.copy(out=ut[:], in_=P0b.rearrange("p (c n) -> p c n", n=128)),
                      r=[pk0], w=[utk])
                self._z_tile(t, ut, [utk], 'wout', KC)
            S.barrier()


    def _mlstm(self, les, l):
        S = self.S
        dr = self.dr
        PS = self.PS
        need_ctx = l < 3
        order = [list(range(NT)), [1, 0] + list(range(NT - 1, NT_CTX - 1, -1))]
        Gall = self.sb(les, "mlG", [128, NT * 16], F32)
        LF = self.sb(les, "mlLF", [128, NT * 16], F32)
        Bc = self.sb(les, "mlBc", [128, NT * 8], F32)
        Vc = self.sb(les, "mlVc", [128, NT * 8], F32)
        hgB = self.sb(les, "mlhg", [128, 2048], F32)
        S.dma(hgB[:], dr['ml_head_g'][0:1, :].partition_broadcast(128), w=['mlhg'])
        with ExitStack() as tes:
            wg = self.sb(tes, "mlwg", [128, KC, 16], BF16)
            gb = self.sb(tes, "mlgb", [128, 16], F32)
            self.load_w(wg[:], 'mlwg', dr['ml_w_in'], 8192, 16)
            S.dma(gb[:], dr['ml_gate_b'][0:1, :].partition_broadcast(128), w=['mlgb'])
            for t in range(NT):
                self.proj_tm(PS[0][:, t * 16:(t + 1) * 16], 'ps0', wg, 'mlwg', t, 0, 16)
            S.dve(lambda e: e.tensor_tensor(out=Gall[:].rearrange("p (t g) -> p t g", g=16),
                                            in0=PS[0][:, 0:NT * 16].rearrange("p (t g) -> p t g", g=16),
                                            in1=gb[:].unsqueeze(1).to_broadcast([128, NT, 16]), op=ALU.add),
                  r=['ps0', 'mlgb'], w=['mlG'])
            S.act(lambda e: e.activation(out=LF[:], in_=Gall[:], func=AF.Exp, scale=-1.0), r=['mlG'], w=['mlLF'])
            S.act(lambda e: e.activation(out=LF[:], in_=LF[:], func=AF.Ln, bias=self.ones_f[:, 0:1]),
                  r=['mlLF', 'ones_f'], w=['mlLF'])
            S.dve(lambda e: e.tensor_scalar_mul(out=LF[:], in0=LF[:], scalar1=-1.0), r=['mlLF'], w=['mlLF'])
            for t in range(NT):
                for d in range(2):
                    c0 = t * 16 + (2 * d + 1) * 4
                    S.pe(lambda e, t=t, d=d, c0=c0: e.matmul(PS[1][:, (t * 2 + d) * 4:(t * 2 + d) * 4 + 4],
                                                             lhsT=self.cfb[d][:, 0:128], rhs=LF[:, c0:c0 + 4],
                                                             start=True, stop=True), r=['cfb', 'mlLF'], w=['ps1'])
            S.act(lambda e: e.copy(out=Bc[:], in_=PS[1][:, 0:NT * 8]), r=['ps1'], w=['mlBc'])
            G4 = Gall[:].rearrange("p (t j h) -> p t j h", j=4, h=4)
            for d in range(2):
                S.dve(lambda e, d=d: e.tensor_tensor(
                    out=Vc[:].rearrange("p (t d h) -> p t d h", d=2, h=4)[:, :, d, :], in0=G4[:, :, 2 * d, :],
                    in1=Bc[:].rearrange("p (t d h) -> p t d h", d=2, h=4)[:, :, d, :], op=ALU.subtract),
                    r=['mlG', 'mlBc'], w=['mlVc'])
            S.act(lambda e: e.activation(out=Vc[:], in_=Vc[:], func=AF.Exp), r=['mlVc'], w=['mlVc'])
            S.barrier()
        def head_body(h):
            with ExitStack() as hes:
                qT = self.sb(hes, "mlqT", [128, 2, T_ALL], BF16)
                kT = self.sb(hes, "mlkT", [128, 2, T_ALL], BF16)
                Kt = self.sb(hes, "mlKt", [128, NT, 256], BF16)
                Vt = self.sb(hes, "mlVt", [128, NT, 520], BF16)
                Hs = self.sb(hes, "mlHs", [128, NT, 512], F32)
                S.pool(lambda e: e.memset(Hs[:], 0.0), w=[('Hs', t) for t in range(NT)])
                S.pool(lambda e: e.memset(Vt[:], 1.0), w=['Vt_init'])
                with ExitStack() as ses:
                    wq = self.sb(ses, "mlwq", [128, KC, 256], BF16)
                    wk = self.sb(ses, "mlwk", [128, KC, 256], BF16)
                    wv = self.sb(ses, "mlwv", [128, KC, 512], BF16)
                    self.load_w(wq[:], 'mlwq', dr['ml_w_in'], h * 256, 256)
                    self.load_w(wk[:], 'mlwk', dr['ml_w_in'], 1024 + h * 256, 256)
                    self.load_w(wv[:], 'mlwv', dr['ml_w_in'], 2048 + h * 512, 512)
                    blocks = [(0, T_CTX)] + [(T_CTX + i * 512, 512) for i in range(4)]
                    nb = 0
                    for (t0, n) in blocks:
                        for c in range(2):
                            for (w_, wk_, dst, sc) in ((wq, 'mlwq', qT, 1.0 / 16), (wk, 'mlwk', kT, 1.0)):
                                pj = nb % 2
                                nb += 1
                                self.proj_fm(PS[pj][:, 0:n], 'ps%d' % pj, w_, wk_, t0, n, c * 128)
                                S.act(lambda e, pj=pj, dst=dst, c=c, t0=t0, n=n, sc=sc: e.mul(
                                    out=dst[:, c, t0:t0 + n], in_=PS[pj][:, 0:n], mul=sc),
                                    r=['ps%d' % pj], w=[('qkT', id(dst), c, t0)])
                    for t in range(NT):
                        self.proj_tm(PS[2][:, 0:256], 'ps2', wk, 'mlwk', t, 0, 256)
                        S.act(lambda e, t=t: e.copy(out=Kt[:, t, :], in_=PS[2][:, 0:256]), r=['ps2'], w=[('Kt', t)])
                        self.proj_tm(PS[3][:, :], 'ps3', wv, 'mlwv', t, 0, 512)
                        S.act(lambda e, t=t: e.copy(out=Vt[:, t, 0:512], in_=PS[3][:, :]), r=['ps3', 'Vt_init'],
                              w=[('Vt', t)])
                    qk_keys = [('qkT', id(dst), c, t0) for dst in (qT, kT) for c in range(2) for (t0, n) in blocks]
                    Cst = [self.sb(ses, "mlC%d" % d, [128, 2, 512], F32) for d in range(2)]
                    Cb = [self.sb(ses, "mlCb%d" % d, [128, 2, 512], BF16) for d in range(2)]
                    Nst = [self.sb(ses, "mlN%d" % d, [128, 2], F32) for d in range(2)]
                    Nb = [self.sb(ses, "mlNb%d" % d, [128, 2], BF16) for d in range(2)]
                    for d in range(2):
                        S.pool(lambda e, d=d: e.memset(Cst[d][:], 0.0), w=[('C', d, 0), ('C', d, 1)])
                        S.pool(lambda e, d=d: e.memset(Cb[d][:], 0.0), w=[('Cb', d)])
                        S.pool(lambda e, d=d: e.memset(Nst[d][:], 0.0), w=[('N', d)])
                        S.pool(lambda e, d=d: e.memset(Nb[d][:], 0.0), w=[('Nb', d)])

                    def wt(nm, shape, dt):
                        return [[self.sb(ses, "%s%d_%d" % (nm, d, b), shape, dt) for b in range(2)] for d in range(2)]
                    ebt_ = wt("mlebt", [128, 128], F32)
                    em_ = wt("mlem", [128, 128], F32)
                    wcol_ = wt("mlw", [128, 4], F32)
                    pt_ = wt("mlpt", [128, 128], BF16)
                    qs_ = wt("mlqs", [128, 2, 128], BF16)
                    kw_ = wt("mlkw", [128, 256], BF16)

                    def chain(step, d):
                        a = order[d][step]
                        par = step % 2
                        Tri = self.cfb[d][:, 0:128]
                        last = 127 if d == 0 else 0
                        dk = lambda nm: ('ml', nm, d, par)
                        ebt, em, wcol, pt, qs, kw = (x[d][par] for x in (ebt_, em_, wcol_, pt_, qs_, kw_))
                        lfc = a * 16 + (2 * d + 1) * 4 + h
                        vcol = Vc[:, a * 8 + d * 4 + h:a * 8 + d * 4 + h + 1]
                        PA, PSt, PB, PU = PS[4 * d], PS[4 * d + 1], PS[4 * d + 2], PS[4 * d + 3]
                        PCD = (PU, PU)
                        kA = lambda nm: 'ps%d' % (4 * d)
                        kS = 'ps%d' % (4 * d + 1)
                        kB = 'ps%d' % (4 * d + 2)
                        kCD = ('ps%d' % (4 * d + 3), 'ps%d' % (4 * d + 3))
                        S.pe(lambda e: e.matmul(PA[:, 0:128], lhsT=LF[:, lfc:lfc + 1].to_broadcast([128, 128]),
                                                rhs=Tri, start=True, stop=True), r=['mlLF', 'cfb'], w=[kA('bt')])
                        S.act(lambda e: e.activation(out=ebt[:], in_=PA[:, 0:128], func=AF.Exp),
                              r=[kA('bt')], w=[dk('ebt')])
                        yield
                        S.pool(lambda e: e.tensor_tensor(out=em[:], in0=ebt[:], in1=Tri, op=ALU.mult),
                               r=[dk('ebt'), 'cfb'], w=[dk('em')])
                        S.dve(lambda e: e.tensor_tensor(out=wcol[:, 0:1], in0=vcol, in1=ebt[:, last:last + 1], op=ALU.mult),
                              r=['mlVc', dk('ebt')], w=[dk('w')])
                        for c in range(2):
                            S.pe(lambda e, c=c: e.matmul(PSt[:, 0:128], lhsT=kT[:, c, a * 128:(a + 1) * 128],
                                                         rhs=qT[:, c, a * 128:(a + 1) * 128],
                                                         start=(c == 0), stop=(c == 1)), r=qk_keys, w=[kS])
                        yield
                        S.dve(lambda e: e.scalar_tensor_tensor(out=pt[:], in0=PSt[:, 0:128], scalar=vcol, in1=em[:],
                                                               op0=ALU.mult, op1=ALU.mult),
                              r=[kS, 'mlVc', dk('em')], w=[dk('pt')])
                        S.dve(lambda e: e.tensor_tensor(out=qs[:], in0=qT[:, :, a * 128:(a + 1) * 128],
                                                        in1=ebt[:].unsqueeze(1).to_broadcast([128, 2, 128]), op=ALU.mult),
                              r=qk_keys + [dk('ebt')], w=[dk('qs')])
                        S.act(lambda e: e.mul(out=kw[:], in_=Kt[:, a, :], mul=wcol[:, 0:1]),
                              r=[('Kt', a), dk('w')], w=[dk('kw')])
                        yield
                        S.pe(lambda e: e.matmul(PB[:, :], lhsT=pt[:], rhs=Vt[:, a, 0:512], start=True, stop=False),
                             r=[dk('pt'), ('Vt', a)], w=[kB])
                        for c in range(2):
                            S.pe(lambda e, c=c: e.matmul(PB[:, :], lhsT=qs[:, c, :], rhs=Cb[d][:, c, :],
                                                         start=False, stop=(c == 1)), r=[dk('qs'), ('Cb', d)], w=[kB])
                        S.pe(lambda e: e.matmul(PA[:, 256:257], lhsT=pt[:], rhs=Vt[:, a, 512:513], start=True, stop=False),
                             r=[dk('pt'), ('Vt', a)], w=[kA('den')])
                        for c in range(2):
                            S.pe(lambda e, c=c: e.matmul(PA[:, 256:257], lhsT=qs[:, c, :], rhs=Nb[d][:, c:c + 1],
                                                         start=False, stop=(c == 1)), r=[dk('qs'), ('Nb', d)], w=[kA('den')])
                        yield
                        S.dve(lambda e: e.tensor_copy(out=wcol[:, 3:4], in_=PA[:, 256:257]), r=[kA('den')], w=[dk('dn')])
                        S.dve(lambda e: e.scalar_tensor_tensor(out=wcol[:, 1:2], in0=wcol[:, 3:4], scalar=-1.0,
                                                               in1=wcol[:, 3:4], op0=ALU.mult, op1=ALU.max),
                              r=[dk('dn')], w=[dk('rd')])
                        S.dve(lambda e: e.tensor_scalar_max(out=wcol[:, 1:2], in0=wcol[:, 1:2], scalar1=1.0),
                              r=[dk('rd')], w=[dk('rd')])
                        S.dve(lambda e: e.reciprocal(out=wcol[:, 2:3], in_=wcol[:, 1:2]), r=[dk('rd')], w=[dk('rd2')])
                        yield
                        S.dve(lambda e: e.scalar_tensor_tensor(out=Hs[:, a, :], in0=PB[:, :], scalar=wcol[:, 2:3],
                                                               in1=Hs[:, a, :], op0=ALU.mult, op1=ALU.add),
                              r=[kB, dk('rd2'), ('Hs', a)], w=[('Hs', a)])
                        yield
                        for c in range(2):
                            S.pe(lambda e, c=c: e.matmul(PU[:, :], lhsT=kw[:, c * 128:(c + 1) * 128],
                                                         rhs=Vt[:, a, 0:512], start=True, stop=True),
                                 r=[dk('kw'), ('Vt', a)], w=[kCD[c]])
                            S.dve(lambda e, c=c: e.scalar_tensor_tensor(
                                out=Cst[d][:, c, :], in0=Cst[d][:, c, :], scalar=ebt[:, last:last + 1],
                                in1=PU[:, :], op0=ALU.mult, op1=ALU.add),
                                r=[('C', d, c), dk('ebt'), kCD[c]], w=[('C', d, c)])
                            S.pool(lambda e, c=c: e.tensor_copy(out=Cb[d][:, c, :], in_=Cst[d][:, c, :]),
                                   r=[('C', d, c)], w=[('Cb', d)])
                            yield
                        for c in range(2):
                            S.pe(lambda e, c=c: e.matmul(PA[:, 264 + c:265 + c], lhsT=kw[:, c * 128:(c + 1) * 128],
                                                         rhs=Vt[:, a, 512:513], start=True, stop=True),
                                 r=[dk('kw'), ('Vt', a)], w=[kA('nu')])
                        S.dve(lambda e: e.scalar_tensor_tensor(out=Nst[d][:], in0=Nst[d][:], scalar=ebt[:, last:last + 1],
                                                               in1=PA[:, 264:266], op0=ALU.mult, op1=ALU.add),
                              r=[('N', d), dk('ebt'), kA('nu')], w=[('N', d)])
                        S.pool(lambda e: e.tensor_copy(out=Nb[d][:], in_=Nst[d][:]), r=[('N', d)], w=[('Nb', d)])
                        yield

                    for step in range(NT):
                        gens = [chain(step, 0), chain(step, 1)]
                        while gens:
                            for g_ in list(gens):
                                try:
                                    next(g_)
                                except StopIteration:
                                    gens.remove(g_)
                    S.barrier()
                with ExitStack() as fes:
                    wo = self.sb(fes, "mlwo", [128, KC, 512], BF16)
                    wz = self.sb(fes, "mlwz", [128, KC, 512], BF16)
                    self.load_w(wo[:], 'mlwo', dr['ml_w_in'], 4096 + h * 512, 512)
                    self.load_w(wz[:], 'mlwz', dr['ml_w_in'], 6144 + h * 512, 512)
                    junk = self.sb(fes, "mljunk", [128, 512], BF16)
                    ss = self.sb(fes, "mlss", [128, 2 * NT], F32)
                    so = self.sb(fes, "mlso", [128, 512], F32)
                    sz = self.sb(fes, "mlsz", [128, 512], F32)
                    hn = self.sb(fes, "mlhn", [128, 512], F32)
                    ub = self.sb(fes, "mlub", [128, 512], BF16)
                    uTt = [self.sb(fes, "mluT%d" % i, [128, 4, 128], BF16) for i in range(2)]
                    tiles = list(range(NT)) if need_ctx else list(range(NT_CTX, NT))
                    for n_, t in enumerate(tiles):
                        S.act(lambda e, t=t: e.activation(out=junk[:], in_=Hs[:, t, :], func=AF.Square,
                                                          accum_out=ss[:, 2 * t:2 * t + 1]),
                              r=[('Hs', t)], w=['mljunk', ('mlss', t)])
                        S.act(lambda e, t=t: e.activation(out=ss[:, 2 * t + 1:2 * t + 2], in_=ss[:, 2 * t:2 * t + 1],
                                                          func=AF.Sqrt, scale=1.0 / 512, bias=self.eps_col[:, 0:1]),
                              r=[('mlss', t), 'eps'], w=[('mlss1', t)])
                        S.dve(lambda e, t=t: e.reciprocal(out=ss[:, 2 * t:2 * t + 1], in_=ss[:, 2 * t + 1:2 * t + 2]),
                              r=[('mlss1', t)], w=[('mlss', t)])
                        self.proj_tm(PS[6][:, :], 'ps6', wo, 'mlwo', t, 0, 512)
                        self.proj_tm(PS[7][:, :], 'ps7', wz, 'mlwz', t, 0, 512)
                        S.act(lambda e: e.activation(out=so[:], in_=PS[6][:, :], func=AF.Sigmoid), r=['ps6'], w=['mlso'])
                        S.act(lambda e: e.activation(out=sz[:], in_=PS[7][:, :], func=AF.Silu), r=['ps7'], w=['mlsz'])
                        S.dve(lambda e, t=t, h=h: e.scalar_tensor_tensor(
                            out=hn[:], in0=Hs[:, t, :], scalar=ss[:, 2 * t:2 * t + 1], in1=hgB[:, h * 512:(h + 1) * 512],
                            op0=ALU.mult, op1=ALU.mult), r=[('Hs', t), ('mlss', t), 'mlhg'], w=['mlhn'])
                        S.pool(lambda e: e.tensor_tensor(out=so[:], in0=so[:], in1=sz[:], op=ALU.mult),
                               r=['mlso', 'mlsz'], w=['mlso'])
                        S.dve(lambda e: e.tensor_tensor(out=ub[:], in0=hn[:], in1=so[:], op=ALU.mult),
                              r=['mlhn', 'mlso'], w=['mlub'])
                        pj = n_ % 2
                        Pb = PS[pj][:].bitcast(BF16)
                        for k in range(4):
                            S.pe(lambda e, Pb=Pb, k=k: e.transpose(out=Pb[:, k * 128:(k + 1) * 128],
                                                                   in_=ub[:, k * 128:(k + 1) * 128], identity=self.ident_b[:]),
                                 r=['mlub', 'ident_b'], w=['ps%d' % pj])
                        ut = uTt[n_ % 2]
                        utk = 'mluT%d' % (n_ % 2)
                        S.act(lambda e, Pb=Pb, ut=ut: e.copy(out=ut[:], in_=Pb[:, 0:512].rearrange("p (c n) -> p c n", n=128)),
                              r=['ps%d' % pj], w=[utk])
                        S.dma(dr['ut'][h * 512:(h + 1) * 512, t * 128:(t + 1) * 128].rearrange("(c p) n -> p c n", p=128),
                              ut[:], r=[utk], w=[('utd', h, t)])
                    S.barrier()
        for h_ in range(4):
            head_body(h_)
        with ExitStack() as tes:
            self.wout = self.sb(tes, "wout", [128, 16, D], BF16)
            self.load_w(self.wout[:], 'wout', dr['ml_w_out'], 0, D, kc=16)
            uin = [self.sb(tes, "mluin%d" % i, [128, 16, 128], BF16) for i in range(2)]
            self._z_begin(tes)
            tiles = list(range(NT)) if need_ctx else list(range(NT_CTX, NT))
            for n_, t in enumerate(tiles):
                u = uin[n_ % 2]
                uk = 'mluin%d' % (n_ % 2)
                S.dma(u[:], dr['ut'][:, t * 128:(t + 1) * 128].rearrange("(c p) n -> p c n", p=128),
                      r=[('utd', hh, t) for hh in range(4)], w=[uk])
                self._z_tile(t, u, [uk], 'wout', 16)
            S.barrier()

    def _conv(self, les, l):
        S = self.S
        dr = self.dr
        PS = self.PS
        need_ctx = l < 3
        hT_ = self.hT
        uT = self.sb(les, "uT", [128, KC, T_ALL], BF16)
        self.wout = self.sb(les, "wout", [128, KC, D], BF16)
        cw = self.sb(les, "cw", [128, KC, 3], F32)
        cb = self.sb(les, "cb", [128, KC, 1], F32)
        for k_ in range(3):
            S.dma(cw[:, :, k_], dr['sc_conv_w'][k_, :].rearrange("(c p) -> p c", p=128), w=['cw'],
                  allow_slow_non_contiguous=True)
        S.dma(cb[:, :, 0], dr['sc_conv_b'][0, :].rearrange("(c p) -> p c", p=128), w=['cb'],
              allow_slow_non_contiguous=True)
        blocks = [(0, T_CTX)] + [(T_CTX + i * 512, 512) for i in range(4)]
        with ExitStack() as tes:
            wc = [self.sb(tes, "wc%d" % i, [128, KC, 512], BF16) for i in range(2)]
            abuf = [self.sb(tes, "abuf%d" % i, [128, T_ALL + 4], F32) for i in range(2)]
            ybuf = [self.sb(tes, "ybuf%d" % i, [128, T_ALL], F32) for i in range(2)]
            xin_sb = [self.sb(tes, "xin%d" % i, [128, 512], F32) for i in range(2)]
            sz = [self.sb(tes, "sz%d" % i, [128, 512], F32) for i in range(2)]
            bsb = [self.sb(tes, "bsb%d" % i, [128, 512], F32) for i in range(2)]
            for i in range(2):
                S.pool(lambda e, i=i: e.memset(abuf[i][:], 0.0), w=['abuf%d' % i])

            def acol(tok):
                return tok + 1 if tok < T_CTX else tok + 3

            for m in range(KC):
                w = wc[m % 2]
                wk = 'wc%d' % (m % 2)
                for q in range(4):
                    self.load_w(w[:, :, q * 128:(q + 1) * 128], wk, dr['sc_w_in'], q * D + m * 128, 128)
                ab = abuf[m % 2]
                abk = 'abuf%d' % (m % 2)
                yb = ybuf[m % 2]
                ybk = 'ybuf%d' % (m % 2)
                for bi, (t0, n) in enumerate(blocks):
                    j = bi % 2
                    Px, Pc = PS[0 + 2 * j], PS[1 + 2 * j]
                    for q, P in ((0, Px), (2, Pc)):
                        for k in range(KC):
                            S.pe(lambda e, P=P, q=q, k=k, t0=t0, n=n, w=w: e.matmul(
                                P[:, 0:n], lhsT=w[:, k, q * 128:(q + 1) * 128], rhs=hT_[:, k, t0:t0 + n],
                                start=(k == 0), stop=(k == KC - 1)),
                                r=[wk] + [('hT', tt) for tt in range(t0 // 128, (t0 + n) // 128)],
                                w=['ps%d' % (q // 2 + 2 * j)])
                    S.act(lambda e, Px=Px, j=j, n=n: e.copy(out=xin_sb[j][:, 0:n], in_=Px[:, 0:n]),
                          r=['ps%d' % (2 * j)], w=['xin%d' % j])
                    c0 = acol(t0)
                    S.dve(lambda e, Pc=Pc, j=j, n=n, c0=c0, ab=ab: e.tensor_tensor(
                        out=ab[:, c0:c0 + n], in0=Pc[:, 0:n], in1=xin_sb[j][:, 0:n], op=ALU.mult),
                        r=['ps%d' % (1 + 2 * j), 'xin%d' % j], w=[abk])
                for (t0, n) in ((0, T_CTX), (T_CTX, T_LAT)):
                    c0 = acol(t0)
                    S.dve(lambda e, ab=ab, yb=yb, c0=c0, n=n, t0=t0, m=m: e.tensor_scalar(
                        out=yb[:, t0:t0 + n], in0=ab[:, c0:c0 + n], scalar1=cw[:, m, 1:2], scalar2=cb[:, m, 0:1],
                        op0=ALU.mult, op1=ALU.add), r=[abk, 'cw', 'cb'], w=[ybk])
                    S.dve(lambda e, ab=ab, yb=yb, c0=c0, n=n, t0=t0, m=m: e.scalar_tensor_tensor(
                        out=yb[:, t0:t0 + n], in0=ab[:, c0 - 1:c0 - 1 + n], scalar=cw[:, m, 0:1], in1=yb[:, t0:t0 + n],
                        op0=ALU.mult, op1=ALU.add), r=[abk, 'cw', ybk], w=[ybk])
                    S.dve(lambda e, ab=ab, yb=yb, c0=c0, n=n, t0=t0, m=m: e.scalar_tensor_tensor(
                        out=yb[:, t0:t0 + n], in0=ab[:, c0 + 1:c0 + 1 + n], scalar=cw[:, m, 2:3], in1=yb[:, t0:t0 + n],
                        op0=ALU.mult, op1=ALU.add), r=[abk, 'cw', ybk], w=[ybk])
                for bi, (t0, n) in enumerate(blocks):
                    j = bi % 2
                    Pb, Pz = PS[4 + 2 * j], PS[5 + 2 * j]
                    for q, P in ((1, Pb), (3, Pz)):
                        for k in range(KC):
                            S.pe(lambda e, P=P, q=q, k=k, t0=t0, n=n, w=w: e.matmul(
                                P[:, 0:n], lhsT=w[:, k, q * 128:(q + 1) * 128], rhs=hT_[:, k, t0:t0 + n],
                                start=(k == 0), stop=(k == KC - 1)),
                                r=[wk] + [('hT', tt) for tt in range(t0 // 128, (t0 + n) // 128)],
                                w=['ps%d' % (4 + q // 2 + 2 * j)])
                    S.act(lambda e, Pz=Pz, j=j, n=n: e.activation(out=sz[j][:, 0:n], in_=Pz[:, 0:n], func=AF.Silu),
                          r=['ps%d' % (5 + 2 * j)], w=['sz%d' % j])
                    S.dve(lambda e, Pb=Pb, j=j, n=n, t0=t0, yb=yb: e.tensor_tensor(
                        out=bsb[j][:, 0:n], in0=Pb[:, 0:n], in1=yb[:, t0:t0 + n], op=ALU.mult),
                        r=['ps%d' % (4 + 2 * j), ybk], w=['bsb%d' % j])
                    S.pool(lambda e, j=j, n=n, t0=t0, m=m: e.tensor_tensor(
                        out=uT[:, m, t0:t0 + n], in0=bsb[j][:, 0:n], in1=sz[j][:, 0:n], op=ALU.mult),
                        r=['bsb%d' % j, 'sz%d' % j], w=[('uT', m, bi)])
            self.load_w(self.wout[:], 'wout', dr['sc_w_out'], 0, D)
            S.barrier()
        tiles = list(range(NT)) if need_ctx else list(range(NT_CTX, NT))

        def uT_fn(t):
            bi = 0 if t < NT_CTX else 1 + (t - NT_CTX) // 4
            return uT[:, :, t * 128:(t + 1) * 128], [('uT', m, bi) for m in range(KC)]
        self._phase_z(les, 'wout', KC, uT_fn, tiles)


def _rope_table():
    rows = T_LAT // 64
    row = np.repeat(np.arange(rows), 64).astype(np.float32)
    col = np.tile(np.arange(64), rows).astype(np.float32)
    n_freq = 16
    freqs = np.power(np.float32(10000.0), -np.arange(n_freq, dtype=np.float32) / np.float32(n_freq)).astype(np.float32)
    ang = np.concatenate([row[:, None] * freqs, col[:, None] * freqs], axis=-1).astype(np.float32)
    return np.concatenate([np.cos(ang), np.sin(ang)], axis=-1).astype(np.float32)


def _const_inputs():
    idx = np.arange(128)
    sel = np.zeros((2, 256), np.float32)
    sel[0, 0:128] = 1.0
    sel[1, 128:256] = 1.0
    return {
        'ident': np.eye(128, dtype=np.float32),
        'cf': np.concatenate([(idx[:, None] <= idx[None, :]), (idx[:, None] > idx[None, :])], axis=1).astype(np.float32),
        'cb': np.concatenate([(idx[:, None] >= idx[None, :]), (idx[:, None] < idx[None, :])], axis=1).astype(np.float32),
        'sel': sel,
        'rope': _rope_table(),
    }


def make_in_maps(inputs, x_all, ctx_all):
    f = lambda a: np.ascontiguousarray(np.asarray(a, dtype=np.float32))
    shared = dict(_const_inputs())
    for name, shape in W_SPECS:
        shared[name] = f(inputs[name]).reshape(shape)
    maps = []
    for b in range(8):
        m = dict(shared)
        m['x'] = f(x_all[b])
        m['ctx'] = f(ctx_all[b])
        m['c2'] = np.ascontiguousarray(np.stack([f(inputs['c'])[b], f(inputs['c_ctx'])], axis=0))
        maps.append(m)
    return maps


_PROG_CACHE = {}


def run_layers(inputs, layers, x_all=None, ctx_all=None, trace=False):
    key = tuple(layers)
    if key not in _PROG_CACHE:
        p = Prog(list(layers))
        p.build()
        _PROG_CACHE[key] = p
    p = _PROG_CACHE[key]
    x_all = inputs['x'] if x_all is None else x_all
    ctx_all = inputs['ctx'] if ctx_all is None else ctx_all
    maps = make_in_maps(inputs, x_all, ctx_all)
    import os
    ncores = int(os.environ.get('KNCORES', '8'))
    res = run_bass_kernel_spmd(p.nc, maps[:ncores], core_ids=list(range(ncores)), **({'trace': True} if trace else {}))
    x_out = np.stack([r['out'] for r in res.results], axis=0)
    ctx_out = np.stack([r['ctxs'] for r in res.results], axis=0)
    return x_out, ctx_out, res


def kernel(**inputs):
    x_out, _, _ = run_layers(inputs, (0, 1, 2, 3))
    return x_out.astype(np.float32)
```
